# Optimizing a Trainium2 kernel written in Bass

```python
import math
import jax, jax.numpy as jnp
from jax import lax
import numpy as np

D_MODEL = 2048
BATCH = 4
SEQ = 2048
DEPTH = 4
DEC_BATCH = 128
DEC_SEQ = 8
PAST_LEN = 16384
PAGE_SIZE = 128

MIX = D_MODEL
W_A = MIX // 2
H_A = 16
HD_A = W_A // H_A
CONV_A = 4
C_LRU = 8.0
W_B = MIX // 4
CONV_B = 31
W_C = MIX // 4
H_C = 8
HD_C = W_C // H_C
CHUNK = 128
EPS = 1e-6
SPLITS = (W_A, W_A, W_B, W_B, W_B, W_C, W_C, W_C)
IN_COLS = sum(SPLITS)

kernel_name = "hybrid_lru_conformer_gmlp_decoder_step"


def rmsnorm(x, g):
    xf = x.astype(jnp.float32)
    y = xf * lax.rsqrt(jnp.mean(xf * xf, axis=-1, keepdims=True) + EPS)
    return y.astype(x.dtype) * g


def layernorm(x, g, b):
    xf = x.astype(jnp.float32)
    mu = jnp.mean(xf, axis=-1, keepdims=True)
    xc = xf - mu
    y = xc * lax.rsqrt(jnp.mean(xc * xc, axis=-1, keepdims=True) + EPS)
    return y.astype(x.dtype) * g + b


def causal_dwconv(x, hist, w, b):
    xc = jnp.concatenate([hist.astype(x.dtype), x], axis=1)
    y = lax.conv_general_dilated(xc, w[:, None, :].astype(x.dtype), window_strides=(1,),
                                 padding='VALID', dimension_numbers=('NWC', 'WIO', 'NWC'),
                                 feature_group_count=x.shape[-1])
    return y + b, xc[:, -(w.shape[0] - 1):]


def rg_lru(x, h0, w_a, b_a, w_i, b_i, lam):
    B, L, _ = x.shape
    xh = x.reshape(B, L, H_A, HD_A)
    r = jax.nn.sigmoid(jnp.einsum('blhd,hde->blhe', xh, w_a).reshape(B, L, W_A) + b_a)
    i = jax.nn.sigmoid(jnp.einsum('blhd,hde->blhe', xh, w_i).reshape(B, L, W_A) + b_i)
    log_a = -C_LRU * r.astype(jnp.float32) * jax.nn.softplus(-lam.astype(jnp.float32))
    a = jnp.exp(log_a)
    bterm = jnp.sqrt(-jnp.expm1(2.0 * log_a)) * (i * x).astype(jnp.float32)
    bterm = bterm.at[:, 0].add(a[:, 0] * h0.astype(jnp.float32))

    def combine(left, right):
        a1, b1 = left
        a2, b2 = right
        return a1 * a2, a2 * b1 + b2

    _, h = lax.associative_scan(combine, (a, bterm), axis=1)
    return h.astype(x.dtype), h[:, -1].astype(h0.dtype)


def chunk_spatial_gate(u, vn, w_s, b_s):
    B, L, _ = u.shape
    n = -(-L // CHUNK)
    pad = n * CHUNK - L
    vp = jnp.pad(vn, ((0, 0), (0, pad), (0, 0))).reshape(B, n, CHUNK, H_C, HD_C)
    mask = jnp.tril(jnp.ones((CHUNK, CHUNK), dtype=bool))
    ws = jnp.where(mask[None], w_s, jnp.zeros_like(w_s))
    mixed = jnp.einsum('hts,bnshd->bnthd', ws, vp) + b_s.T[None, None, :, :, None]
    mixed = mixed.reshape(B, n * CHUNK, W_C)[:, :L]
    return u * mixed


def layer(x, c, hist_a, h0, hist_b,
          norm_g, w_ada, b_ada, w_in, conv_a_w, conv_a_b, lru_wa, lru_ba, lru_wi, lru_bi, lru_lam,
          conv_b_w, conv_b_b, ln_b_g, ln_b_b, ln_c_g, ln_c_b, gmlp_ws, gmlp_bs, w_out):
    mod = jax.nn.silu(c) @ w_ada + b_ada
    shift, scale, gate = jnp.split(mod[:, None, :], 3, axis=-1)
    xn = rmsnorm(x, norm_g) * (1.0 + scale) + shift
    z = xn @ w_in
    idx = list(np.cumsum(SPLITS)[:-1])
    xa, ga, gl_a, gl_b, gb, u, v, gc = jnp.split(z, idx, axis=-1)
    xa_c, new_hist_a = causal_dwconv(xa, hist_a, conv_a_w, conv_a_b)
    h, h_last = rg_lru(xa_c, h0, lru_wa, lru_ba, lru_wi, lru_bi, lru_lam)
    ya = h * jax.nn.silu(ga)
    glu = gl_a * jax.nn.sigmoid(gl_b)
    cb, new_hist_b = causal_dwconv(glu, hist_b, conv_b_w, conv_b_b)
    yb = jax.nn.silu(layernorm(cb, ln_b_g, ln_b_b)) * jax.nn.silu(gb)
    vn = layernorm(v, ln_c_g, ln_c_b)
    yc = chunk_spatial_gate(u, vn, gmlp_ws, gmlp_bs) * jax.nn.silu(gc)
    out = jnp.concatenate([ya, yb, yc], axis=-1) @ w_out
    return x + gate * out, new_hist_a, h_last, new_hist_b, vn


def setup_inputs(seed: int = 0) -> dict:
    key = jax.random.key(seed)
    ks = iter(jax.random.split(key, 40))
    f32 = jnp.float32
    nrm = lambda shape, s: jax.random.normal(next(ks), shape, f32) * s
    a0 = jax.random.uniform(next(ks), (DEPTH, W_A), f32, 0.9, 0.999)
    return {
        "x_prompt": nrm((BATCH, SEQ, D_MODEL), 1.0),
        "x_sample": nrm((DEC_BATCH, DEC_SEQ, D_MODEL), 1.0),
        "c_prompt": nrm((BATCH, D_MODEL), 1.0),
        "c_sample": nrm((DEC_BATCH, D_MODEL), 1.0),
        "state_lru_conv": nrm((DEPTH, DEC_BATCH, CONV_A - 1, W_A), 1.0),
        "state_lru_h": nrm((DEPTH, DEC_BATCH, W_A), 0.5),
        "state_ccm_conv": nrm((DEPTH, DEC_BATCH, CONV_B - 1, W_B), 0.5),
        "norm_g": 1.0 + nrm((DEPTH, D_MODEL), 0.02),
        "w_ada": nrm((DEPTH, D_MODEL, 3 * D_MODEL), 0.5 * D_MODEL ** -0.5),
        "b_ada": nrm((DEPTH, 3 * D_MODEL), 0.02),
        "w_in": nrm((DEPTH, D_MODEL, IN_COLS), D_MODEL ** -0.5),
        "conv_a_w": nrm((DEPTH, CONV_A, W_A), CONV_A ** -0.5),
        "conv_a_b": nrm((DEPTH, W_A), 0.02),
        "lru_wa": nrm((DEPTH, H_A, HD_A, HD_A), HD_A ** -0.5),
        "lru_ba": nrm((DEPTH, W_A), 0.02),
        "lru_wi": nrm((DEPTH, H_A, HD_A, HD_A), HD_A ** -0.5),
        "lru_bi": nrm((DEPTH, W_A), 0.02),
        "lru_lam": jnp.log(a0) - jnp.log1p(-a0),
        "conv_b_w": nrm((DEPTH, CONV_B, W_B), CONV_B ** -0.5),
        "conv_b_b": nrm((DEPTH, W_B), 0.02),
        "ln_b_g": 1.0 + nrm((DEPTH, W_B), 0.02),
        "ln_b_b": nrm((DEPTH, W_B), 0.02),
        "ln_c_g": 1.0 + nrm((DEPTH, W_C), 0.02),
        "ln_c_b": nrm((DEPTH, W_C), 0.02),
        "gmlp_ws": nrm((DEPTH, H_C, CHUNK, CHUNK), CHUNK ** -0.5),
        "gmlp_bs": 1.0 + nrm((DEPTH, H_C, CHUNK), 0.02),
        "w_out": nrm((DEPTH, MIX, D_MODEL), MIX ** -0.5),
        "final_g": 1.0 + nrm((D_MODEL,), 0.02),
    }


def reference(x_prompt, x_sample, c_prompt, c_sample, state_lru_conv, state_lru_h, state_ccm_conv,
              norm_g, w_ada, b_ada, w_in, conv_a_w, conv_a_b, lru_wa, lru_ba, lru_wi, lru_bi, lru_lam,
              conv_b_w, conv_b_b, ln_b_g, ln_b_b, ln_c_g, ln_c_b, gmlp_ws, gmlp_bs, w_out, final_g):
    bp = x_prompt.shape[0]
    dt = x_prompt.dtype
    xp, xs = x_prompt, x_sample
    conv_a_p, h_p, conv_b_p = [], [], []
    conv_a_s, h_s, conv_b_s, v_s = [], [], [], []
    for l in range(DEPTH):
        params = (norm_g[l], w_ada[l], b_ada[l], w_in[l], conv_a_w[l], conv_a_b[l], lru_wa[l], lru_ba[l],
                  lru_wi[l], lru_bi[l], lru_lam[l], conv_b_w[l], conv_b_b[l], ln_b_g[l], ln_b_b[l],
                  ln_c_g[l], ln_c_b[l], gmlp_ws[l], gmlp_bs[l], w_out[l])
        xp, ha, hl, hb, _ = layer(xp, c_prompt,
                                  jnp.zeros((bp, CONV_A - 1, W_A), dt), jnp.zeros((bp, W_A), dt),
                                  jnp.zeros((bp, CONV_B - 1, W_B), dt), *params)
        conv_a_p.append(ha); h_p.append(hl); conv_b_p.append(hb)
        xs, ha, hl, hb, vn = layer(xs, c_sample, state_lru_conv[l], state_lru_h[l], state_ccm_conv[l], *params)
        conv_a_s.append(ha); h_s.append(hl); conv_b_s.append(hb); v_s.append(vn)
    y_prompt = rmsnorm(xp, final_g)
    y_sample = rmsnorm(xs, final_g)
    return (y_prompt, y_sample,
            jnp.stack(conv_a_p), jnp.stack(h_p), jnp.stack(conv_b_p),
            jnp.stack(conv_a_s), jnp.stack(h_s), jnp.stack(conv_b_s), jnp.stack(v_s))
```

```python
import contextlib
import numpy as np
import concourse.bass as bass
import concourse.mybir as mybir
from concourse.bass_utils import run_bass_kernel_spmd

F32 = mybir.dt.float32
BF16 = mybir.dt.bfloat16
ALU = mybir.AluOpType
AF = mybir.ActivationFunctionType
PE, ACT, DVE, POOL, SP = "tensor", "scalar", "vector", "gpsimd", "sync"

L = 4
D = 2048
KC = 16
T = 512
NTILE = 4
SEQ = 2048
NSQ = 16
ST = 8
NS = 1 + NSQ
EPS = 1e-6
NB = 3
WG = 256

O_NG, O_BADA, O_CAW, O_CAB, O_BA, O_BI, O_LAM, O_CBW, O_CBB, O_LBG, O_LBB, NP = \
    0, 16, 64, 96, 104, 112, 120, 128, 252, 256, 260, 264
D_NBA, D_NBI, D_C, D_C2, ND = 0, 8, 16, 24, 32

IN_OFF = {"xa": 0, "ga": 1024, "gla": 2048, "glb": 2560, "gb": 3072, "u": 3584, "v": 4096, "gc": 4608}


class Op:
    __slots__ = ("eng", "fn", "deps", "is_dma", "sem", "val", "signal", "idx", "dkey")

    def __init__(self, eng, fn, is_dma, dkey):
        self.eng = eng
        self.fn = fn
        self.deps = []
        self.is_dma = is_dma
        self.sem = None
        self.val = 0
        self.signal = False
        self.dkey = dkey


class Graph:
    def __init__(self):
        self.ops = []
        self.last_w = {}
        self.readers = {}

    def op(self, eng, fn, reads=(), writes=(), dma=None):
        o = Op(eng, fn, dma is not None, dma)
        o.idx = len(self.ops)
        deps = {}
        for k in reads:
            w = self.last_w.get(k)
            if w is not None:
                deps[w.idx] = (w, True)
        for k in writes:
            w = self.last_w.get(k)
            if w is not None and w.idx not in deps:
                deps[w.idx] = (w, False)
            for r in self.readers.get(k, ()):
                if r.idx not in deps:
                    deps[r.idx] = (r, False)
        for w, raw in deps.values():
            if (not w.is_dma) and (not o.is_dma) and w.eng == o.eng:
                if not raw:
                    continue
                if o.eng == PE:
                    continue
            o.deps.append(w)
            w.signal = True
        for k in reads:
            self.readers.setdefault(k, []).append(o)
        for k in writes:
            self.last_w[k] = o
            self.readers[k] = []
        self.ops.append(o)
        return o

    def finalize(self):
        for o in self.ops:
            best = {}
            for d in o.deps:
                k = ("dma", d.dkey) if d.is_dma else ("eng", d.eng)
                if k not in best or d.idx > best[k].idx:
                    best[k] = d
            o.deps = list(best.values())
        for o in self.ops:
            o.signal = o.is_dma
        for o in self.ops:
            for d in o.deps:
                d.signal = True
        eng_cnt = {}
        dma_cnt = {}
        for o in self.ops:
            if o.is_dma:
                o.signal = True
                c = dma_cnt.get(o.dkey, 0) + 16
                dma_cnt[o.dkey] = c
                o.sem = ("dma", o.dkey)
                o.val = c
            elif o.signal:
                c = eng_cnt.get(o.eng, 0) + 1
                eng_cnt[o.eng] = c
                o.sem = ("eng", o.eng)
                o.val = c
        per_eng = {}
        for o in self.ops:
            per_eng.setdefault(o.eng, []).append(o)
        return per_eng, dma_cnt


def emit_graph(nc, g, st):
    per_eng, dma_cnt = g.finalize()
    keys = sorted({o.sem for o in g.ops if o.signal}, key=str)
    sems = {k: st.enter_context(nc.semaphore(f"sm{i}")) for i, k in enumerate(keys)}
    block = st.enter_context(nc.Block())
    finals = [(("dma", k), v) for k, v in dma_cnt.items()]

    def mk(engname, ops):
        def body(e):
            waited = {}
            for o in ops:
                for d in o.deps:
                    if waited.get(d.sem, 0) >= d.val:
                        continue
                    e.wait_ge(sems[d.sem], d.val)
                    waited[d.sem] = d.val
                ins = o.fn(e)
                if o.signal:
                    ins.then_inc(sems[o.sem], 16 if o.is_dma else 1)
            if engname == SP:
                for sem, val in finals:
                    if waited.get(sem, 0) < val:
                        e.wait_ge(sems[sem], val)
        return body

    for engname in (SP, ACT, DVE, POOL, PE):
        ops = per_eng.get(engname, [])
        if not ops and engname != SP:
            continue
        getattr(block, engname)(mk(engname, ops))


class Ring:
    def __init__(self, tiles, name):
        self.tiles = tiles
        self.name = name
        self.i = 0

    def get(self):
        i = self.i % len(self.tiles)
        self.i += 1
        return self.tiles[i], (self.name, i)


def build_program(with_sample=True):
    nc = bass.Bass("TRN2", target_bir_lowering=False)
    g = Graph()

    def din(name, shape):
        return nc.dram_tensor(name, list(shape), F32, kind="ExternalInput").ap()

    def dout(name, shape):
        return nc.dram_tensor(name, list(shape), F32, kind="ExternalOutput").ap()

    xpT = din("xpT", [D, SEQ])
    cT = din("cT", [128, KC, NS])
    pvec = din("pvec", [128, L, NP])
    fgv = din("fgv", [128, KC])
    w_ada = din("w_ada", [L, D, 3 * D])
    w_in = din("w_in", [L, D, 5120])
    w_out = din("w_out", [L, D, D])
    wblk_d = din("wblk", [L, 128, 2, 8, 128])
    wsT_d = din("wsT", [L, 8, 128, 128])
    bsx_d = din("bsx", [L, 128, 4, 128])
    lnc_d = din("lnc", [L, 1, 1024])
    wcache = nc.dram_tensor("wcache", [L, 28, 128, KC * WG], BF16, kind="Internal").ap()
    ypT = dout("ypT", [D, SEQ])
    lcp = dout("lcp", [128, L, 8, 3])
    lhp = dout("lhp", [128, L, 8])
    ccp = dout("ccp", [128, L, 4, 30])
    if with_sample:
        xsT = din("xsT", [128, KC, NSQ * ST])
        slc = din("slc", [128, L, 8, NSQ, 3])
        slh = din("slh", [128, L, 8, NSQ])
        scc = din("scc", [128, L, 4, NSQ, 30])
        ysT = dout("ysT", [128, KC, NSQ * ST])
        lcs = dout("lcs", [128, L, 8, NSQ, 3])
        lhs = dout("lhs", [128, L, 8, NSQ])
        ccs = dout("ccs", [128, L, 4, NSQ, 30])
        gvs = dout("gvs", [L, NSQ * ST, 512])
        ws8_d = din("ws8x", [L, 128, 8, 128])
        mask8_d = din("mask8", [128, 128])

    with contextlib.ExitStack() as st:
        def sb(name, shape, dt=F32):
            return st.enter_context(nc.sbuf_tensor(name, list(shape), dt))

        def ps(name, shape, dt=F32):
            return st.enter_context(nc.psum_tensor(name, list(shape), dt))

        xT = sb("xT", [128, KC, T])
        xn = sb("xn", [128, KC, T], BF16)
        yb = sb("yb", [128, KC, T], BF16)
        wb = [sb(f"wb{i}", [128, KC, WG], BF16) for i in range(NB)]
        ring = Ring([sb(f"rg{i}", [128, T]) for i in range(6)], "rg")
        ringb = Ring([sb(f"rb{i}", [128, T], BF16) for i in range(4)], "rb")
        xcr = Ring([sb(f"xcr{i}", [128, T]) for i in range(3)], "xcr")
        hold4 = sb("hold4", [128, 4, T])
        cb4 = sb("cb4", [128, 4, T])
        xa_ext = Ring([sb(f"xae{i}", [128, T + 3]) for i in range(2)], "xae")
        GEW = max(T + 30, NSQ * (ST + 30))
        glu_ext = Ring([sb(f"gle{i}", [128, GEW]) for i in range(1)], "gle")
        glu_bf = Ring([sb(f"glb{i}", [128, GEW], BF16) for i in range(2)], "glb")
        dgw = Ring([sb(f"dgw{i}", [128, 31, 128], BF16) for i in range(2)], "dgw")
        ident = sb("ident", [128, 128], BF16)
        vn = sb("vn", [128, 4, 512], BF16)
        lncb = sb("lncb", [128, 2, 512])
        rstd = sb("rstd", [128, T])
        rstdB = sb("rstdB", [128, T])
        meanB = sb("meanB", [128, T])
        PV = sb("PV", [128, L, NP])
        PV2 = sb("PV2", [128, L, ND])
        FG = sb("FG", [128, KC])
        modT = sb("modT", [128, L, 48, NS])
        ct = ring.tiles[0][:, 0:KC * NS]
        cth = ring.tiles[1][:, 0:KC * NS]
        scb = sb("scb", [128, KC, NS], BF16)
        wblk = sb("wblk_s", [128, 2, 8, 128], BF16)
        wsT = sb("wsT_s", [128, 8, 128], BF16)
        bsT = sb("bsT", [128, 4, 128])
        ones_d = sb("ones_d", [128, 128], BF16)
        ones_b = sb("ones_b", [128, 128], BF16)
        hist_a = sb("hist_a", [128, L, 8, 3])
        hst = sb("hst", [128, L, 8])
        hist_b = sb("hist_b", [128, L, 4, 30])
        if with_sample:
            sst_a = sb("sst_a", [128, 8, NSQ, 3])
            sst_h = sb("sst_h", [128, 8, NSQ])
            sst_c = sb("sst_c", [128, 4, NSQ, 30])
            BD = sb("BD", [128, 8, 128], BF16)
            mask8 = sb("mask8_s", [128, 128], BF16)
        cst = sb("cst", [128, 2])
        EPSC = cst[:, 0:1]
        ONE = cst[:, 1:2]
        small = Ring([sb(f"sm{i}", [128, 16]) for i in range(8)], "sm")
        psA = [ps(f"psA{i}", [128, 512]) for i in range(4)]
        psB = [ps(f"psB{i}", [128, 512]) for i in range(4)]

        def act(out, in_, func, bias=0.0, scale=1.0, r=(), w=()):
            g.op(ACT, lambda e: e.activation(out, in_, func, bias=bias, scale=scale), reads=r, writes=w)

        def stt(out, in0, scalar, in1, op0, op1, r=(), w=(), eng=DVE):
            g.op(eng, lambda e: e.scalar_tensor_tensor(out, in0, scalar, in1, op0, op1), reads=r, writes=w)

        def ts(out, in0, s1, s2, op0, op1=None, r=(), w=(), eng=DVE):
            if op1 is None:
                g.op(eng, lambda e: e.tensor_scalar(out, in0, s1, None, op0), reads=r, writes=w)
            else:
                g.op(eng, lambda e: e.tensor_scalar(out, in0, s1, s2, op0, op1), reads=r, writes=w)

        def tt(out, in0, in1, op, r=(), w=(), eng=DVE):
            g.op(eng, lambda e: e.tensor_tensor(out, in0, in1, op), reads=r, writes=w)

        def cp(out, in_, r=(), w=(), eng=DVE):
            g.op(eng, lambda e: e.tensor_copy(out, in_), reads=r, writes=w)

        def mm(out, lhsT, rhs, start, stop, r=(), w=()):
            g.op(PE, lambda e: e.matmul(out, lhsT, rhs, start=start, stop=stop), reads=r, writes=w)

        def dma(eng, out, in_, r=(), w=(), key=None):
            g.op(eng, lambda e: e.dma_start(out=out, in_=in_), reads=r, writes=w, dma=key)

        def sigm(dst, src, srckeys, dkey, nbias=0.0, extra=()):
            act(dst, src, AF.Exp, bias=nbias, scale=-1.0, r=list(srckeys) + list(extra), w=[dkey])
            act(dst, dst, AF.Ln, bias=ONE, r=[dkey, "cst"], w=[dkey])
            act(dst, dst, AF.Exp, scale=-1.0, r=[dkey], w=[dkey])

        def rsqrt_act(dst, src, srckeys, dkey, bias=0.0):
            act(dst, src, AF.Ln, bias=bias, r=list(srckeys) + ["cst"], w=[dkey])
            act(dst, dst, AF.Exp, scale=-0.5, r=[dkey], w=[dkey])

        wcount = [0]

        wmode = ["normal"]

        def load_w(src2d, tag=None):
            i = wcount[0] % NB
            wcount[0] += 1
            if wmode[0] == "cached" and tag is not None:
                dma(SP, wb[i][:].rearrange("p k n -> p (k n)"), wcache[tag[0], tag[1]], r=[("wc", tag)],
                    w=[("wb", i)], key=("w", i))
                return wb[i], ("wb", i)
            dma(POOL, wb[i][:], src2d.rearrange("(k p) n -> p k n", p=128), w=[("wb", i)], key=("w", i))
            if wmode[0] == "store" and tag is not None:
                dma(SP, wcache[tag[0], tag[1]], wb[i][:].rearrange("p k n -> p (k n)"), r=[("wb", i)],
                    w=[("wc", tag)], key=("wcs", i))
            return wb[i], ("wb", i)

        def load_layer_params(l):
            dma(POOL, wblk[:], wblk_d[l], w=["wblk"], key="wblk")
            dma(POOL, wsT[:], wsT_d[l].rearrange("h s t -> s h t"), w=["wsT"], key="wsT")
            g.op(POOL, lambda e: e.affine_select(out=wsT[:], in_=wsT[:], pattern=[[0, 8], [1, 128]],
                                                 compare_op=ALU.is_ge, fill=0.0, base=0, channel_multiplier=-1),
                 reads=["wsT"], writes=["wsT"])
            dma(SP, bsT[:], bsx_d[l], w=["bsT"], key="bsT")
            dma(SP, lncb[:].rearrange("p q n -> p (q n)"), lnc_d[l].to_broadcast([128, 1024]), w=["lncb"], key="lncb")

        dma(SP, PV[:], pvec, w=["PV"], key="PV")
        dma(SP, FG[:], fgv, w=["FG"], key="FG")
        dma(SP, ct, cT.rearrange("p k s -> p (k s)"), w=[("rg", 0)], key="ct")
        g.op(DVE, lambda e: e.memset(ident[:], 1.0), writes=["ident"])
        g.op(POOL, lambda e: e.affine_select(out=ident[:], in_=ident[:], pattern=[[1, 128]], compare_op=ALU.is_equal,
                                             fill=0.0, base=0, channel_multiplier=-1), reads=["ident"], writes=["ident"])
        g.op(DVE, lambda e: e.memset(ones_d[:], 1.0 / D), writes=["ones"])
        g.op(DVE, lambda e: e.memset(ones_b[:], 1.0 / 512), writes=["ones"])
        g.op(DVE, lambda e: e.memset(hist_a[:], 0.0), writes=["hist_a"])
        g.op(DVE, lambda e: e.memset(hst[:], 0.0), writes=["hst"])
        g.op(DVE, lambda e: e.memset(hist_b[:], 0.0), writes=["hist_b"])
        g.op(DVE, lambda e: e.memset(cst[:, 0:1], EPS), writes=["cst"])
        g.op(DVE, lambda e: e.memset(cst[:, 1:2], 1.0), writes=["cst"])
        for l in range(L):
            ts(PV2[:, l, D_NBA:D_NBA + 8], PV[:, l, O_BA:O_BA + 8], -1.0, None, ALU.mult, r=["PV"], w=["PV2"])
            ts(PV2[:, l, D_NBI:D_NBI + 8], PV[:, l, O_BI:O_BI + 8], -1.0, None, ALU.mult, r=["PV"], w=["PV2"])
            sm1, k1 = small.get()
            act(sm1[:, 0:8], PV[:, l, O_LAM:O_LAM + 8], AF.Exp, scale=-1.0, r=["PV"], w=[k1])
            act(sm1[:, 0:8], sm1[:, 0:8], AF.Ln, bias=ONE, r=[k1, "cst"], w=[k1])
            ts(PV2[:, l, D_C:D_C + 8], sm1[:, 0:8], -8.0, None, ALU.mult, r=[k1], w=["PV2"])
            ts(PV2[:, l, D_C2:D_C2 + 8], sm1[:, 0:8], -16.0, None, ALU.mult, r=[k1], w=["PV2"])

        sigm(cth, ct, [("rg", 0)], ("rg", 1))
        tt(scb[:].rearrange("p k s -> p (k s)"), cth, ct, ALU.mult, r=[("rg", 0), ("rg", 1)], w=["scb"])
        NG_ADA = 3 * D // WG
        CPG = WG // 128
        for l in range(L):
            for gi in range(NG_ADA):
                wt, wk = load_w(w_ada[l][:, gi * WG:(gi + 1) * WG])
                pb = psB[gi % 4]
                pk = ("psB", gi % 4)
                for m4 in range(CPG):
                    for k in range(KC):
                        mm(pb[:, m4 * 32:m4 * 32 + NS], wt[:, k, m4 * 128:(m4 + 1) * 128], scb[:, k, :],
                           k == 0, k == KC - 1, r=[wk, "scb"], w=[pk])
                tt(modT[:, l, gi * CPG:(gi + 1) * CPG, :],
                   pb[:, 0:32 * CPG].rearrange("p (a b) -> p a b", b=32)[:, :, 0:NS],
                   PV[:, l, O_BADA + gi * CPG:O_BADA + (gi + 1) * CPG].unsqueeze(2).to_broadcast([128, CPG, NS]),
                   ALU.add, r=[pk, "PV"], w=["mod"])
            ts(modT[:, l, 16:32, :], modT[:, l, 16:32, :], 1.0, None, ALU.add, r=["mod"], w=["mod"])
            tt(modT[:, l, 16:32, :], modT[:, l, 16:32, :],
               PV[:, l, O_NG:O_NG + 16].unsqueeze(2).to_broadcast([128, 16, NS]), ALU.mult, r=["mod", "PV"], w=["mod"])

        class Cx:
            pass

        PCX = Cx()
        PCX.N, PCX.nseq, PCX.tps, PCX.s0, PCX.sample = T, 1, T, 0, False
        SCX = Cx()
        SCX.N, SCX.nseq, SCX.tps, SCX.s0, SCX.sample = NSQ * ST, NSQ, ST, 1, True

        class Pipe:
            def __init__(self):
                self.items = []

            def add(self, stages):
                self.items.append(stages)

            def run(self):
                n = len(self.items)
                if n == 0:
                    return
                maxlag = max(lg for it in self.items for lg, _ in it)
                for t in range(n + maxlag):
                    todo = []
                    for i in range(max(0, t - maxlag), min(n, t + 1)):
                        for lg, fn in self.items[i]:
                            if i + lg == t:
                                todo.append((-lg, i, fn))
                    todo.sort(key=lambda x: (x[0], x[1]))
                    for _, _, fn in todo:
                        fn()
                self.items = []

        zcnt = [0]

        def rms_sq(cx, k, first, last):
            N = cx.N
            sq, sk = ringb.get()
            act(sq[:, :N], xT[:, k, :N], AF.Square, r=[("x", k)], w=[sk])
            mm(psB[0][:, :N], ones_d[:], sq[:, :N], first, last, r=[sk, "ones"], w=[("psB", 0)])

        def rms_fin(cx):
            rsqrt_act(rstd[:, :cx.N], psB[0][:, :cx.N], [("psB", 0)], "rstd", bias=EPSC)

        def run_layer(cx, l):
            N, nseq, tps, s0 = cx.N, cx.nseq, cx.tps, cx.s0
            pvs = lambda off: PV[:, l, off:off + 1]
            pv2 = lambda off: PV2[:, l, off:off + 1]
            pipe = Pipe()

            def V(ap):
                return ap.rearrange("p (s t) -> p s t", t=tps)

            def bcs(idx):
                return modT[:, l, idx, s0:s0 + nseq].unsqueeze(2).to_broadcast([128, nseq, tps])

            load_layer_params(l)
            if cx.sample:
                dma(POOL, BD[:], ws8_d[l], w=["BD"], key="BD")
                tt(BD[:], BD[:], mask8[:].unsqueeze(1).to_broadcast([128, 8, 128]), ALU.mult,
                   r=["BD", "mask8"], w=["BD"])
                dma(SP, sst_a[:], slc[:, l], w=["sst_a"], key="sst_a")
                dma(SP, sst_h[:], slh[:, l], w=["sst_h"], key="sst_h")
                dma(SP, sst_c[:], scc[:, l], w=["sst_c"], key="sst_c")
                ha = lambda j: sst_a[:, j, :, :]
                h0 = lambda j: sst_h[:, j, :]
                hb = lambda jj: sst_c[:, jj, :, :]
                hak, h0k, hbk = "sst_a", "sst_h", "sst_c"
            else:
                ha = lambda j: hist_a[:, l, j, :].unsqueeze(1)
                h0 = lambda j: hst[:, l, j:j + 1]
                hb = lambda jj: hist_b[:, l, jj, :].unsqueeze(1)
                hak, h0k, hbk = "hist_a", "hst", "hist_b"

            for k in range(KC):
                t, tk = ring.get()
                if cx.sample:
                    tt(t[:, :N], xT[:, k, :N], rstd[:, :N], ALU.mult, r=[("x", k), "rstd"], w=[tk])
                    tt(V(t[:, :N]), V(t[:, :N]), bcs(16 + k), ALU.mult, r=[tk, "mod"], w=[tk])
                    tt(V(xn[:, k, :N]), V(t[:, :N]), bcs(k), ALU.add, r=[tk, "mod"], w=[("xn", k)])
                else:
                    stt(t[:, :N], xT[:, k, :N], modT[:, l, 16 + k, 0:1], rstd[:, :N], ALU.mult, ALU.mult,
                        r=[("x", k), "rstd", "mod"], w=[tk])
                    act(xn[:, k, :N], t[:, :N], AF.Identity, bias=modT[:, l, k, 0:1], r=[tk, "mod"],
                        w=[("xn", k)])
            wl = w_in[l]
            itemno = [0]

            def zitems(name, gidx, mk_stages):
                c0 = IN_OFF[name] + gidx * WG
                box = {}

                def ld():
                    box["w"] = load_w(wl[:, c0:c0 + WG], tag=(l, c0 // WG))

                for jl in range(CPG):
                    S = Cx()
                    S.c = gidx * CPG + jl
                    S.par = itemno[0] % 2
                    itemno[0] += 1

                    def zfn(S=S, jl=jl, first=(jl == 0)):
                        if first:
                            ld()
                        wt, wk = box["w"]
                        b = zcnt[0] % 4
                        zcnt[0] += 1
                        S.zp, S.zk = psA[b][:, :N], ("psA", b)
                        for k in range(KC):
                            mm(S.zp, wt[:, k, jl * 128:(jl + 1) * 128], xn[:, k, :N], k == 0, k == KC - 1,
                               r=[wk, ("xn", k)], w=[S.zk])
                    pipe.add([(0, zfn)] + mk_stages(S))

            def sig_den(dst, src, skeys, dkey, nbias=0.0, extra=()):
                act(dst, src, AF.Exp, bias=nbias, scale=-1.0, r=list(skeys) + list(extra), w=[dkey])
                ts(dst, dst, 1.0, None, ALU.add, r=[dkey], w=[dkey])

            def st_xa(S):
                j = S.c
                jj = j % 4
                gb_ = (psB[1], psB[2]) if S.par == 0 else (psB[0], psB[3])
                gk_ = (("psB", 1), ("psB", 2)) if S.par == 0 else (("psB", 0), ("psB", 3))

                def front():
                    xet, xk = xa_ext.get()
                    xe = xet[:, 0:nseq * (3 + tps)].rearrange("p (s t) -> p s t", t=3 + tps)
                    cp(xe[:, :, 0:3], ha(j), r=[hak], w=[xk])
                    cp(xe[:, :, 3:3 + tps], V(S.zp), r=[S.zk], w=[xk])
                    cp(ha(j), xe[:, :, tps:tps + 3], r=[xk], w=[hak])
                    xct, S.xck = xcr.get()
                    S.xc = xct[:, :N]
                    ts(V(S.xc), xe[:, :, 0:tps], pvs(O_CAW + j * 4), pvs(O_CAB + j), ALU.mult, ALU.add,
                       r=[xk, "PV"], w=[S.xck])
                    for kk in range(1, 4):
                        stt(V(S.xc), xe[:, :, kk:kk + tps], pvs(O_CAW + j * 4 + kk), V(S.xc), ALU.mult, ALU.add,
                            r=[xk, S.xck, "PV"], w=[S.xck])
                    xcbt, S.xcbk = ringb.get()
                    S.xcb = xcbt[:, :N]
                    cp(S.xcb, S.xc, r=[S.xck], w=[S.xcbk])

                def gates():
                    mm(gb_[0][:, :N], wblk[:, 0, j, :], S.xcb, True, True, r=["wblk", S.xcbk], w=[gk_[0]])
                    mm(gb_[1][:, :N], wblk[:, 1, j, :], S.xcb, True, True, r=["wblk", S.xcbk], w=[gk_[1]])

                def back():
                    trt, trk = ring.get()
                    tr = trt[:, :N]
                    sigm(tr, gb_[0][:, :N], [gk_[0]], trk, nbias=pv2(D_NBA + j), extra=["PV2"])
                    at, ak = ring.get()
                    a = at[:, :N]
                    act(a, tr, AF.Exp, scale=pv2(D_C + j), r=[trk, "PV2"], w=[ak])
                    act(tr, tr, AF.Exp, scale=pv2(D_C2 + j), r=[trk, "PV2"], w=[trk])
                    act(tr, tr, AF.Ln, bias=ONE, scale=-1.0, r=[trk, "cst"], w=[trk])
                    act(tr, tr, AF.Exp, scale=0.5, r=[trk], w=[trk])
                    tit, tik = ring.get()
                    ti = tit[:, :N]
                    sigm(ti, gb_[1][:, :N], [gk_[1]], tik, nbias=pv2(D_NBI + j), extra=["PV2"])
                    tt(ti, ti, S.xc, ALU.mult, r=[tik, S.xck], w=[tik])
                    tt(ti, ti, tr, ALU.mult, r=[tik, trk], w=[tik])
                    if nseq == 1:
                        g.op(DVE, lambda e: e.tensor_tensor_scan(
                            hold4[:, jj, :N], a, ti, h0(j)[:, 0:1], ALU.mult, ALU.add),
                            reads=[ak, tik, h0k], writes=[("hold4", jj)])
                    else:
                        smt, smk = small.get()
                        tt(smt[:, 0:nseq].unsqueeze(2), V(a)[:, :, 0:1], h0(j).unsqueeze(2), ALU.mult,
                           r=[ak, h0k], w=[smk])
                        tt(V(ti)[:, :, 0:1], V(ti)[:, :, 0:1], smt[:, 0:nseq].unsqueeze(2), ALU.add,
                           r=[tik, smk], w=[tik])
                        g.op(DVE, lambda e: e.memset(V(a)[:, :, 0:1], 0.0), reads=[smk], writes=[ak])
                        g.op(DVE, lambda e: e.tensor_tensor_scan(hold4[:, jj, :N], a, ti, 0.0, ALU.mult, ALU.add),
                             reads=[ak, tik], writes=[("hold4", jj)])
                    cp(h0(j), V(hold4[:, jj, :N])[:, :, tps - 1], r=[("hold4", jj)], w=[h0k])

                return [(1, front), (2, gates), (3, back)]

            def st_ga(S):
                j = S.c
                jj = j % 4

                def post():
                    tgt, tgk = ring.get()
                    tg = tgt[:, :N]
                    sigm(tg, S.zp, [S.zk], tgk)
                    tt(tg, tg, S.zp, ALU.mult, r=[tgk, S.zk], w=[tgk])
                    tt(yb[:, j, :N], tg, hold4[:, jj, :N], ALU.mult, r=[tgk, ("hold4", jj)], w=[("y", j)])
                return [(4, post)]

            def st_gla(S):
                def post():
                    act(hold4[:, S.c, :N], S.zp, AF.Copy, r=[S.zk], w=[("hold4", S.c)])
                return [(1, post)]

            def st_glb(S):
                jj = S.c
                cb_ps = psB[1 + (jj % 2)]
                cbpk = ("psB", 1 + (jj % 2))

                def front():
                    tbt, tbk = ring.get()
                    tb = tbt[:, :N]
                    sigm(tb, S.zp, [S.zk], tbk)
                    get_, gk = glu_ext.get()
                    ge = get_[:, 0:nseq * (30 + tps)].rearrange("p (s t) -> p s t", t=30 + tps)
                    gbt, S.gbk = glu_bf.get()
                    S.geb = gbt[:, 0:nseq * (30 + tps)].rearrange("p (s t) -> p s t", t=30 + tps)
                    cp(ge[:, :, 0:30], hb(jj), r=[hbk], w=[gk])
                    tt(ge[:, :, 30:30 + tps], V(hold4[:, jj, :N]), V(tb), ALU.mult,
                       r=[tbk, ("hold4", jj)], w=[gk])
                    cp(hb(jj), ge[:, :, tps:tps + 30], r=[gk], w=[hbk])
                    cp(gbt[:, 0:nseq * (30 + tps)], get_[:, 0:nseq * (30 + tps)], r=[gk], w=[S.gbk])
                    S.dgt, S.dgk = dgw.get()
                    tt(S.dgt[:], ident[:].unsqueeze(1).to_broadcast([128, 31, 128]),
                       PV[:, l, O_CBW + jj * 31:O_CBW + jj * 31 + 31].unsqueeze(2).to_broadcast([128, 31, 128]),
                       ALU.mult, r=["ident", "PV"], w=[S.dgk])

                def conv():
                    for kk in range(31):
                        mm(V(cb_ps[:, :N]), S.dgt[:, kk, :], S.geb[:, :, kk:kk + tps], kk == 0, kk == 30,
                           r=[S.dgk, S.gbk], w=[cbpk])

                def post_conv():
                    cbk = ("cb4", jj)
                    act(cb4[:, jj, :N], cb_ps[:, :N], AF.Identity, bias=pvs(O_CBB + jj), r=[cbpk, "PV"], w=[cbk])
                    cbf, S.cbfk = ringb.get()
                    S.cbf = cbf
                    act(cbf[:, :N], cb4[:, jj, :N], AF.Copy, r=[cbk], w=[S.cbfk])
                    cbs, S.cbsk = ringb.get()
                    S.cbs = cbs
                    act(cbs[:, :N], cb4[:, jj, :N], AF.Square, r=[cbk], w=[S.cbsk])

                def stats():
                    mm(psB[0][:, :N], ones_b[:], S.cbf[:, :N], jj == 0, jj == 3, r=[S.cbfk, "ones"], w=[("psB", 0)])
                    mm(psB[3][:, :N], ones_b[:], S.cbs[:, :N], jj == 0, jj == 3, r=[S.cbsk, "ones"], w=[("psB", 3)])

                def fin():
                    m2t, m2k = ring.get()
                    m2 = m2t[:, :N]
                    act(m2, psB[0][:, :N], AF.Square, r=[("psB", 0)], w=[m2k])
                    act(meanB[:, :N], psB[0][:, :N], AF.Copy, r=[("psB", 0)], w=["meanB"])
                    tt(m2, psB[3][:, :N], m2, ALU.subtract, r=[("psB", 3), m2k], w=[m2k])
                    ts(m2, m2, 0.0, EPS, ALU.max, ALU.add, r=[m2k], w=[m2k])
                    S.m2, S.m2k = m2, m2k

                def fin_b():
                    rsqrt_act(rstdB[:, :N], S.m2, [S.m2k], "rstdB")

                st = [(1, front), (2, conv), (3, post_conv), (4, stats)]
                if jj == 3:
                    st += [(5, fin), (6, fin_b)]
                return st

            def st_gb(S):
                jj = S.c

                def pre():
                    S.tgbk = ("hold4", jj)
                    S.tgb = hold4[:, jj, :N]
                    sigm(S.tgb, S.zp, [S.zk], S.tgbk)
                    tt(S.tgb, S.tgb, S.zp, ALU.mult, r=[S.tgbk, S.zk], w=[S.tgbk])

                def post_a():
                    xct, S.xck = ring.get()
                    S.xc = xct[:, :N]
                    tt(S.xc, cb4[:, jj, :N], meanB[:, :N], ALU.subtract, r=[("cb4", jj), "meanB"], w=[S.xck])
                    tt(S.xc, S.xc, rstdB[:, :N], ALU.mult, r=[S.xck, "rstdB"], w=[S.xck])

                def post_b():
                    xc, xck = S.xc, S.xck
                    lnt, lnk = ring.get()
                    ln = lnt[:, :N]
                    act(ln, xc, AF.Identity, bias=pvs(O_LBB + jj), scale=pvs(O_LBG + jj), r=[xck, "PV"], w=[lnk])
                    sigm(xc, ln, [lnk], xck)
                    tt(ln, ln, xc, ALU.mult, r=[xck, lnk], w=[lnk])
                    tt(yb[:, 8 + jj, :N], ln, S.tgb, ALU.mult, r=[lnk, S.tgbk], w=[("y", 8 + jj)])
                return [(1, pre), (7, post_a), (8, post_b)]

            for g4 in range(4):
                zitems("xa", g4, st_xa)
                zitems("ga", g4, st_ga)
            for gx in range(2):
                zitems("gla", gx, st_gla)
            for gx in range(2):
                zitems("glb", gx, st_glb)
            for gx in range(2):
                zitems("gb", gx, st_gb)
            pipe.run()

            nq = N // 128
            for hv in range(2):
                c0 = IN_OFF["v"] + hv * WG
                wt, wk = load_w(wl[:, c0:c0 + WG], tag=(l, c0 // WG))
                for q in range(nq):
                    for k in range(KC):
                        mm(psA[q][:, hv * WG:(hv + 1) * WG], xn[:, k, q * 128:(q + 1) * 128], wt[:, k, :],
                           k == 0, k == KC - 1, r=[wk, ("xn", k)], w=[("psA", q)])
            zcnt[0] = nq

            def vnorm(q):
                zp, zk = psA[q], ("psA", q)
                s6, s6k = small.get()
                g.op(DVE, lambda e: e.bn_stats(s6[:, 0:6], zp[:]), reads=[zk], writes=[s6k])
                mv, mvk = small.get()
                g.op(DVE, lambda e: e.bn_aggr(mv[:, 0:2], s6[:, 0:6]), reads=[s6k], writes=[mvk])
                rsqrt_act(mv[:, 2:3], mv[:, 1:2], [mvk], mvk, bias=EPSC)
                vh, vhk = ring.get()
                ts(vh[:], zp[:], mv[:, 0:1], mv[:, 2:3], ALU.subtract, ALU.mult, r=[zk, mvk], w=[vhk])
                tt(vh[:], vh[:], lncb[:, 0, :], ALU.mult, r=[vhk, "lncb"], w=[vhk])
                if cx.sample:
                    tt(vh[:], vh[:], lncb[:, 1, :], ALU.add, r=[vhk, "lncb"], w=[vhk])
                    dma(SP, gvs[l], vh[:], r=[vhk], key=("yout", vhk))
                    act(vn[:, q, :], vh[:], AF.Copy, r=[vhk], w=[("vn", q)])
                else:
                    tt(vn[:, q, :], vh[:], lncb[:, 1, :], ALU.add, r=[vhk, "lncb"], w=[("vn", q)])

            def st_u(S):
                def post():
                    act(hold4[:, S.c, :N], S.zp, AF.Copy, r=[S.zk], w=[("hold4", S.c)])
                return [(1, post)]

            def st_gc(S):
                jj = S.c
                mb = psB[1 + (jj % 2)]
                mbk = ("psB", 1 + (jj % 2))

                def pre():
                    tgct, S.tgck = ring.get()
                    S.tgc = tgct[:, :N]
                    sigm(S.tgc, S.zp, [S.zk], S.tgck)
                    tt(S.tgc, S.tgc, S.zp, ALU.mult, r=[S.tgck, S.zk], w=[S.tgck])

                def mix():
                    for q in range(nq):
                        for h2 in range(2):
                            hd = 2 * jj + h2
                            wmix = BD[:, hd, :] if cx.sample else wsT[:, hd, :]
                            mm(mb[h2 * 64:(h2 + 1) * 64, q * 128:(q + 1) * 128],
                               vn[:, q, hd * 64:(hd + 1) * 64], wmix, True, True,
                               r=[("vn", q), "BD" if cx.sample else "wsT"], w=[mbk])

                def post_mix():
                    m1t, m1k = ring.get()
                    m1 = m1t[:, :N]
                    if cx.sample:
                        bias_bc = bsT[:, jj, 0:ST].unsqueeze(1).to_broadcast([128, NSQ, ST])
                        tt(V(m1), V(mb[:, :N]), bias_bc, ALU.add, r=[mbk, "bsT"], w=[m1k])
                    else:
                        tt(m1.rearrange("p (q t) -> p q t", t=128), mb[:, :N].rearrange("p (q t) -> p q t", t=128),
                           bsT[:, jj, :].unsqueeze(1).to_broadcast([128, nq, 128]), ALU.add,
                           r=[mbk, "bsT"], w=[m1k])
                    tt(m1, m1, hold4[:, jj, :N], ALU.mult, r=[m1k, ("hold4", jj)], w=[m1k])
                    tt(yb[:, 12 + jj, :N], m1, S.tgc, ALU.mult, r=[m1k, S.tgck], w=[("y", 12 + jj)])
                return [(1, pre), (2, mix), (3, post_mix)]

            for q in range(nq):
                vnorm(q)
            for gx in range(2):
                zitems("u", gx, st_u)
            for gx in range(2):
                zitems("gc", gx, st_gc)
            pipe.run()

            for og in range(D // WG):
                box = {}
                for m4 in range(CPG):
                    S = Cx()
                    S.m = og * CPG + m4

                    def zfn(S=S, m4=m4, og=og, box=box):
                        if m4 == 0:
                            box["w"] = load_w(w_out[l][:, og * WG:(og + 1) * WG], tag=(l, 20 + og))
                        wt, wk = box["w"]
                        b = zcnt[0] % 4
                        zcnt[0] += 1
                        S.zp, S.zk = psA[b][:, :N], ("psA", b)
                        for k in range(KC):
                            mm(S.zp, wt[:, k, m4 * 128:(m4 + 1) * 128], yb[:, k, :N], k == 0, k == KC - 1,
                               r=[wk, ("y", k)], w=[S.zk])

                    def resid(S=S):
                        m = S.m
                        if cx.sample:
                            ot, otk = ring.get()
                            tt(V(ot[:, :N]), V(S.zp), bcs(32 + m), ALU.mult, r=[S.zk, "mod"], w=[otk])
                            tt(xT[:, m, :N], xT[:, m, :N], ot[:, :N], ALU.add, r=[otk, ("x", m)], w=[("x", m)])
                        else:
                            stt(xT[:, m, :N], S.zp, modT[:, l, 32 + m, 0:1], xT[:, m, :N], ALU.mult, ALU.add,
                                r=[S.zk, ("x", m), "mod"], w=[("x", m)])
                        sq, S.sk = ringb.get()
                        S.sq = sq
                        act(sq[:, :N], xT[:, m, :N], AF.Square, r=[("x", m)], w=[S.sk])

                    def ssq(S=S):
                        mm(psB[0][:, :N], ones_d[:], S.sq[:, :N], S.m == 0, S.m == KC - 1,
                           r=[S.sk, "ones"], w=[("psB", 0)])
                    pipe.add([(0, zfn), (1, resid), (2, ssq)])
            pipe.run()
            rms_fin(cx)
            if cx.sample:
                dma(SP, lcs[:, l], sst_a[:], r=["sst_a"], key="sst_a")
                dma(SP, lhs[:, l], sst_h[:], r=["sst_h"], key="sst_h")
                dma(SP, ccs[:, l], sst_c[:], r=["sst_c"], key="sst_c")

        def run_tile(cx, load_fn, store_fn):
            load_fn()
            for k in range(KC):
                rms_sq(cx, k, k == 0, k == KC - 1)
            rms_fin(cx)
            for l in range(L):
                run_layer(cx, l)
            stg = [xn[:].rearrange("p k t -> p (k t)").bitcast(F32), yb[:].rearrange("p k t -> p (k t)").bitcast(F32)]
            for k in range(KC):
                buf = stg[k // 8]
                c = k % 8
                t = buf[:, c * T:c * T + cx.N]
                key = "xn" if k < 8 else "y"
                stt(t, xT[:, k, :cx.N], FG[:, k:k + 1], rstd[:, :cx.N], ALU.mult, ALU.mult,
                    r=[("x", k), "rstd", "FG"], w=[(key, 2 * c), (key, 2 * c + 1)])
                if k % 4 == 3:
                    c0 = c - 3
                    src = buf[:, c0 * T:(c0 + 4) * T].rearrange("p (k t) -> p k t", t=T)[:, :, :cx.N]
                    store_fn(k // 4, src, [(key, 2 * c0 + i) for i in range(8)])

        QS = (SP, ACT)
        for p in range(NTILE):
            wmode[0] = "store" if (with_sample and p == NTILE - 1) else "normal"
            run_tile(PCX,
                     lambda p=p: [dma(QS[g4 % 2], xT[:, 4 * g4:4 * g4 + 4, :],
                                      xpT[g4 * 512:(g4 + 1) * 512, p * T:(p + 1) * T].rearrange("(k p) t -> p k t", p=128),
                                      w=[("x", 4 * g4 + i) for i in range(4)], key=("xin", g4)) for g4 in range(4)],
                     lambda g4, src, keys, p=p: dma(QS[g4 % 2],
                                                    ypT[g4 * 512:(g4 + 1) * 512, p * T:(p + 1) * T].rearrange(
                                                        "(k p) t -> p k t", p=128),
                                                    src, r=keys, key=("yst", g4)))
        if with_sample:
            dma(POOL, mask8[:], mask8_d, w=["mask8"], key="mask8")
            wmode[0] = "cached"
            run_tile(SCX,
                     lambda: [dma(QS[g4 % 2], xT[:, 4 * g4:4 * g4 + 4, :SCX.N], xsT[:, 4 * g4:4 * g4 + 4, :],
                                  w=[("x", 4 * g4 + i) for i in range(4)], key=("xin", g4)) for g4 in range(4)],
                     lambda g4, src, keys: dma(QS[g4 % 2], ysT[:, 4 * g4:4 * g4 + 4, :], src, r=keys, key=("yst", g4)))

        dma(SP, lcp, hist_a[:], r=["hist_a"], key="sout")
        dma(SP, lhp, hst[:], r=["hst"], key="sout")
        dma(SP, ccp, hist_b[:], r=["hist_b"], key="sout")

        emit_graph(nc, g, st)
        build_program.sbuf_left = nc.sbuf_bytes_remaining
    return nc


_NC_CACHE = {}


def _pack_pvec(norm_g, b_ada, conv_a_w, conv_a_b, lru_ba, lru_bi, lru_lam, conv_b_w, conv_b_b, ln_b_g, ln_b_b):
    pv = np.zeros((128, L, NP), np.float32)
    fm = lambda v, nch: v.reshape(nch, 128).T
    for l in range(L):
        pv[:, l, O_NG:O_NG + 16] = fm(norm_g[l], 16)
        pv[:, l, O_BADA:O_BADA + 48] = fm(b_ada[l], 48)
        pv[:, l, O_CAW:O_CAW + 32] = conv_a_w[l].reshape(4, 8, 128).transpose(2, 1, 0).reshape(128, 32)
        pv[:, l, O_CAB:O_CAB + 8] = fm(conv_a_b[l], 8)
        pv[:, l, O_BA:O_BA + 8] = fm(lru_ba[l], 8)
        pv[:, l, O_BI:O_BI + 8] = fm(lru_bi[l], 8)
        pv[:, l, O_LAM:O_LAM + 8] = fm(lru_lam[l], 8)
        pv[:, l, O_CBW:O_CBW + 124] = conv_b_w[l].reshape(31, 4, 128).transpose(2, 1, 0).reshape(128, 124)
        pv[:, l, O_CBB:O_CBB + 4] = fm(conv_b_b[l], 4)
        pv[:, l, O_LBG:O_LBG + 4] = fm(ln_b_g[l], 4)
        pv[:, l, O_LBB:O_LBB + 4] = fm(ln_b_b[l], 4)
    return pv


def _pack_wblk(wa, wi):
    out = np.zeros((L, 128, 2, 8, 128), np.float32)
    for which, w in enumerate((wa, wi)):
        for j in range(8):
            for half in range(2):
                out[:, half * 64:(half + 1) * 64, which, j, half * 64:(half + 1) * 64] = w[:, 2 * j + half]
    return out


def _pack_bsx(bs):
    out = np.zeros((L, 128, 4, 128), np.float32)
    for jj in range(4):
        for half in range(2):
            out[:, half * 64:(half + 1) * 64, jj, :] = bs[:, 2 * jj + half][:, None, :]
    return out


def _pack_ws8x(ws):
    out = np.zeros((L, 128, 8, 128), np.float32)
    blk = ws[:, :, :ST, :ST].transpose(0, 3, 1, 2)
    for sq in range(NSQ):
        out[:, sq * ST:(sq + 1) * ST, :, sq * ST:(sq + 1) * ST] = blk
    return out


def _mask8():
    m = np.zeros((128, 128), np.float32)
    for sq in range(NSQ):
        for tp in range(ST):
            m[sq * ST + tp, sq * ST + tp:(sq + 1) * ST] = 1.0
    return m


def kernel(x_prompt, x_sample, c_prompt, c_sample, state_lru_conv, state_lru_h, state_ccm_conv,
           norm_g, w_ada, b_ada, w_in, conv_a_w, conv_a_b, lru_wa, lru_ba, lru_wi, lru_bi, lru_lam,
           conv_b_w, conv_b_b, ln_b_g, ln_b_b, ln_c_g, ln_c_b, gmlp_ws, gmlp_bs, w_out, final_g):
    f = lambda a: np.ascontiguousarray(np.asarray(a, dtype=np.float32))
    x_prompt, x_sample, c_prompt, c_sample = f(x_prompt), f(x_sample), f(c_prompt), f(c_sample)
    state_lru_conv, state_lru_h, state_ccm_conv = f(state_lru_conv), f(state_lru_h), f(state_ccm_conv)
    with_sample = True
    if "nc" not in _NC_CACHE:
        _NC_CACHE["nc"] = build_program(with_sample)
    nc = _NC_CACHE["nc"]
    pv = _pack_pvec(*[np.asarray(a, np.float32) for a in
                      (norm_g, b_ada, conv_a_w, conv_a_b, lru_ba, lru_bi, lru_lam, conv_b_w, conv_b_b, ln_b_g, ln_b_b)])
    fg = f(np.asarray(final_g, np.float32).reshape(16, 128).T)
    shared = {
        "pvec": pv, "fgv": fg, "w_ada": f(w_ada), "w_in": f(w_in), "w_out": f(w_out),
        "wblk": _pack_wblk(np.asarray(lru_wa, np.float32), np.asarray(lru_wi, np.float32)),
        "wsT": f(np.asarray(gmlp_ws, np.float32).transpose(0, 1, 3, 2)),
        "bsx": _pack_bsx(np.asarray(gmlp_bs, np.float32)),
        "ws8x": _pack_ws8x(np.asarray(gmlp_ws, np.float32)), "mask8": _mask8(),
        "lnc": f(np.concatenate([np.asarray(ln_c_g), np.asarray(ln_c_b)], axis=1).reshape(L, 1, 1024)),
    }
    in_maps = []
    for c in range(8):
        b = c % 4
        cs = np.concatenate([c_prompt[b:b + 1], c_sample[NSQ * c:NSQ * (c + 1)]], axis=0)
        m = dict(shared)
        m["xpT"] = f(x_prompt[b].T)
        m["cT"] = f(cs.T.reshape(KC, 128, NS).transpose(1, 0, 2))
        sl = slice(NSQ * c, NSQ * (c + 1))
        m["xsT"] = f(x_sample[sl].reshape(NSQ * ST, KC, 128).transpose(2, 1, 0))
        m["slc"] = f(state_lru_conv[:, sl].reshape(L, NSQ, 3, 8, 128).transpose(4, 0, 3, 1, 2))
        m["slh"] = f(state_lru_h[:, sl].reshape(L, NSQ, 8, 128).transpose(3, 0, 2, 1))
        m["scc"] = f(state_ccm_conv[:, sl].reshape(L, NSQ, 30, 4, 128).transpose(4, 0, 3, 1, 2))
        in_maps.append(m)
    res = run_bass_kernel_spmd(nc, in_maps, core_ids=list(range(8)))
    R = res.results
    B = 4
    y_prompt = np.stack([R[b]["ypT"].T for b in range(B)])
    lc_p = np.stack([R[b]["lcp"].transpose(1, 3, 2, 0).reshape(L, 3, 1024) for b in range(B)], axis=1)
    lh_p = np.stack([R[b]["lhp"].transpose(1, 2, 0).reshape(L, 1024) for b in range(B)], axis=1)
    cc_p = np.stack([R[b]["ccp"].transpose(1, 3, 2, 0).reshape(L, 30, 512) for b in range(B)], axis=1)
    y_sample = np.concatenate([R[c]["ysT"].transpose(2, 1, 0).reshape(NSQ, ST, D) for c in range(8)], axis=0)
    lc_s = np.concatenate([R[c]["lcs"].transpose(1, 3, 4, 2, 0).reshape(L, NSQ, 3, 1024) for c in range(8)], axis=1)
    lh_s = np.concatenate([R[c]["lhs"].transpose(1, 3, 2, 0).reshape(L, NSQ, 1024) for c in range(8)], axis=1)
    cc_s = np.concatenate([R[c]["ccs"].transpose(1, 3, 4, 2, 0).reshape(L, NSQ, 30, 512) for c in range(8)], axis=1)
    gv_s = np.concatenate([R[c]["gvs"].reshape(L, NSQ, ST, 512) for c in range(8)], axis=1)
    y_sample, lc_s, lh_s, cc_s, gv_s = [np.ascontiguousarray(a) for a in (y_sample, lc_s, lh_s, cc_s, gv_s)]
    return (np.ascontiguousarray(y_prompt), y_sample, np.ascontiguousarray(lc_p), np.ascontiguousarray(lh_p),
            np.ascontiguousarray(cc_p), lc_s, lh_s, cc_s, gv_s)
```

```python
import contextlib
import numpy as np
import concourse.bass as bass
import concourse.mybir as mybir
from concourse.bass_utils import run_bass_kernel_spmd

F32 = mybir.dt.float32
BF16 = mybir.dt.bfloat16
ALU = mybir.AluOpType
AF = mybir.ActivationFunctionType
PE, ACT, DVE, POOL, SP = "tensor", "scalar", "vector", "gpsimd", "sync"

L = 4
D = 2048
KC = 16
T = 512
NTILE = 4
SEQ = 2048
NSQ = 16
ST = 8
NS = 1 + NSQ
EPS = 1e-6
NB = 3
WG = 256

O_NG, O_BADA, O_CAW, O_CAB, O_BA, O_BI, O_LAM, O_CBW, O_CBB, O_LBG, O_LBB, NP = \
    0, 16, 64, 96, 104, 112, 120, 128, 252, 256, 260, 264
D_NBA, D_NBI, D_C, D_C2, D_HBA, D_HBI, D_CH, ND = 0, 8, 16, 24, 32, 40, 48, 56

IN_OFF = {"xa": 0, "ga": 1024, "gla": 2048, "glb": 2560, "gb": 3072, "u": 3584, "v": 4096, "gc": 4608}


class Op:
    __slots__ = ("eng", "fn", "deps", "is_dma", "sem", "val", "signal", "idx", "dkey")

    def __init__(self, eng, fn, is_dma, dkey):
        self.eng = eng
        self.fn = fn
        self.deps = []
        self.is_dma = is_dma
        self.sem = None
        self.val = 0
        self.signal = False
        self.dkey = dkey


class Graph:
    def __init__(self):
        self.ops = []
        self.last_w = {}
        self.readers = {}

    def op(self, eng, fn, reads=(), writes=(), dma=None):
        o = Op(eng, fn, dma is not None, dma)
        o.idx = len(self.ops)
        deps = {}
        for k in reads:
            w = self.last_w.get(k)
            if w is not None:
                deps[w.idx] = (w, True)
        for k in writes:
            w = self.last_w.get(k)
            if w is not None and w.idx not in deps:
                deps[w.idx] = (w, False)
            for r in self.readers.get(k, ()):
                if r.idx not in deps:
                    deps[r.idx] = (r, False)
        for w, raw in deps.values():
            if (not w.is_dma) and (not o.is_dma) and w.eng == o.eng:
                if o.eng == PE:
                    continue
                if (not raw) and o.eng == POOL:
                    continue
            o.deps.append(w)
            w.signal = True
        for k in reads:
            self.readers.setdefault(k, []).append(o)
        for k in writes:
            self.last_w[k] = o
            self.readers[k] = []
        self.ops.append(o)
        return o

    def finalize(self):
        for o in self.ops:
            best = {}
            for d in o.deps:
                k = ("dma", d.dkey) if d.is_dma else ("eng", d.eng)
                if k not in best or d.idx > best[k].idx:
                    best[k] = d
            o.deps = list(best.values())
        for o in self.ops:
            o.signal = o.is_dma
        for o in self.ops:
            for d in o.deps:
                d.signal = True
        eng_cnt = {}
        dma_cnt = {}
        for o in self.ops:
            if o.is_dma:
                o.signal = True
                c = dma_cnt.get(o.dkey, 0) + 16
                dma_cnt[o.dkey] = c
                o.sem = ("dma", o.dkey)
                o.val = c
            elif o.signal:
                c = eng_cnt.get(o.eng, 0) + 1
                eng_cnt[o.eng] = c
                o.sem = ("eng", o.eng)
                o.val = c
        per_eng = {}
        for o in self.ops:
            per_eng.setdefault(o.eng, []).append(o)
        return per_eng, dma_cnt


def emit_graph(nc, g, st):
    per_eng, dma_cnt = g.finalize()
    keys = sorted({o.sem for o in g.ops if o.signal}, key=str)
    sems = {k: st.enter_context(nc.semaphore(f"sm{i}")) for i, k in enumerate(keys)}
    block = st.enter_context(nc.Block())
    finals = [(("dma", k), v) for k, v in dma_cnt.items()]

    def mk(engname, ops):
        def body(e):
            waited = {}
            for o in ops:
                for d in o.deps:
                    if waited.get(d.sem, 0) >= d.val:
                        continue
                    e.wait_ge(sems[d.sem], d.val)
                    waited[d.sem] = d.val
                ins = o.fn(e)
                if o.signal:
                    ins.then_inc(sems[o.sem], 16 if o.is_dma else 1)
            if engname == SP:
                for sem, val in finals:
                    if waited.get(sem, 0) < val:
                        e.wait_ge(sems[sem], val)
        return body

    for engname in (SP, ACT, DVE, POOL, PE):
        ops = per_eng.get(engname, [])
        if not ops and engname != SP:
            continue
        getattr(block, engname)(mk(engname, ops))


class Ring:
    def __init__(self, tiles, name):
        self.tiles = tiles
        self.name = name
        self.i = 0

    def get(self):
        i = self.i % len(self.tiles)
        self.i += 1
        return self.tiles[i], (self.name, i)


def build_program(with_sample=True):
    nc = bass.Bass("TRN2", target_bir_lowering=False)
    g = Graph()

    def din(name, shape):
        return nc.dram_tensor(name, list(shape), F32, kind="ExternalInput").ap()

    def dout(name, shape):
        return nc.dram_tensor(name, list(shape), F32, kind="ExternalOutput").ap()

    xpT = din("xpT", [D, SEQ])
    cT = din("cT", [128, KC, NS])
    pvec = din("pvec", [128, L, NP])
    fgv = din("fgv", [128, KC])
    w_ada = din("w_ada", [L, D, 3 * D])
    w_in = din("w_in", [L, D, 5120])
    w_out = din("w_out", [L, D, D])
    wblk_d = din("wblk", [L, 128, 2, 8, 128])
    wsT_d = din("wsT", [L, 8, 128, 128])
    bsx_d = din("bsx", [L, 128, 4, 128])
    lnc_d = din("lnc", [L, 1, 1024])
    wcache = nc.dram_tensor("wcache", [L, 28, 128, KC * WG], BF16, kind="Internal").ap()
    ypT = dout("ypT", [D, SEQ])
    lcp = dout("lcp", [128, L, 8, 3])
    lhp = dout("lhp", [128, L, 8])
    ccp = dout("ccp", [128, L, 4, 30])
    if with_sample:
        xsT = din("xsT", [128, KC, NSQ * ST])
        slc = din("slc", [128, L, 8, NSQ, 3])
        slh = din("slh", [128, L, 8, NSQ])
        scc = din("scc", [128, L, 4, NSQ, 30])
        ysT = dout("ysT", [128, KC, NSQ * ST])
        lcs = dout("lcs", [128, L, 8, NSQ, 3])
        lhs = dout("lhs", [128, L, 8, NSQ])
        ccs = dout("ccs", [128, L, 4, NSQ, 30])
        gvs = dout("gvs", [L, NSQ * ST, 512])
        ws8_d = din("ws8x", [L, 128, 8, 128])
        mask8_d = din("mask8", [128, 128])

    with contextlib.ExitStack() as st:
        def sb(name, shape, dt=F32):
            return st.enter_context(nc.sbuf_tensor(name, list(shape), dt))

        def ps(name, shape, dt=F32):
            return st.enter_context(nc.psum_tensor(name, list(shape), dt))

        xT = sb("xT", [128, KC, T])
        xn = sb("xn", [128, KC, T], BF16)
        yb = sb("yb", [128, KC, T], BF16)
        wb = [sb(f"wb{i}", [128, KC, WG], BF16) for i in range(NB)]
        ring = Ring([sb(f"rg{i}", [128, T]) for i in range(6)], "rg")
        ringb = Ring([sb(f"rb{i}", [128, T], BF16) for i in range(4)], "rb")
        xcr = Ring([sb(f"xcr{i}", [128, T]) for i in range(3)], "xcr")
        hold4 = sb("hold4", [128, 4, T])
        cb4 = sb("cb4", [128, 4, T])
        xa_ext = Ring([sb(f"xae{i}", [128, T + 3]) for i in range(2)], "xae")
        GEW = max(T + 30, NSQ * (ST + 30))
        glu_ext = Ring([sb(f"gle{i}", [128, GEW]) for i in range(1)], "gle")
        glu_bf = Ring([sb(f"glb{i}", [128, GEW], BF16) for i in range(2)], "glb")
        dgw = Ring([sb(f"dgw{i}", [128, 31, 128], BF16) for i in range(2)], "dgw")
        ident = sb("ident", [128, 128], BF16)
        vn = sb("vn", [128, 4, 512], BF16)
        lncb = sb("lncb", [128, 2, 512])
        rstd = sb("rstd", [128, T])
        rstdB = sb("rstdB", [128, T])
        meanB = sb("meanB", [128, T])
        PV = sb("PV", [128, L, NP])
        PV2 = sb("PV2", [128, L, ND])
        FG = sb("FG", [128, KC])
        modT = sb("modT", [128, L, 48, NS])
        ct = ring.tiles[0][:, 0:KC * NS]
        cth = ring.tiles[1][:, 0:KC * NS]
        scb = sb("scb", [128, KC, NS], BF16)
        wblk = sb("wblk_s", [128, 2, 8, 128], BF16)
        wsT = sb("wsT_s", [128, 8, 128], BF16)
        bsT = sb("bsT", [128, 4, 128])
        ones_d = sb("ones_d", [128, 128], BF16)
        ones_b = sb("ones_b", [128, 128], BF16)
        hist_a = sb("hist_a", [128, L, 8, 3])
        hst = sb("hst", [128, L, 8])
        hist_b = sb("hist_b", [128, L, 4, 30])
        if with_sample:
            sst_a = sb("sst_a", [128, 8, NSQ, 3])
            sst_h = sb("sst_h", [128, 8, NSQ])
            sst_c = sb("sst_c", [128, 4, NSQ, 30])
            BD = sb("BD", [128, 8, 128], BF16)
            mask8 = sb("mask8_s", [128, 128], BF16)
        cst = sb("cst", [128, 2])
        EPSC = cst[:, 0:1]
        ONE = cst[:, 1:2]
        small = Ring([sb(f"sm{i}", [128, 16]) for i in range(8)], "sm")
        psA = [ps(f"psA{i}", [128, 512]) for i in range(4)]
        psB = [ps(f"psB{i}", [128, 512]) for i in range(4)]

        def act(out, in_, func, bias=0.0, scale=1.0, r=(), w=()):
            g.op(ACT, lambda e: e.activation(out, in_, func, bias=bias, scale=scale), reads=r, writes=w)

        def stt(out, in0, scalar, in1, op0, op1, r=(), w=(), eng=DVE):
            g.op(eng, lambda e: e.scalar_tensor_tensor(out, in0, scalar, in1, op0, op1), reads=r, writes=w)

        def ts(out, in0, s1, s2, op0, op1=None, r=(), w=(), eng=DVE):
            if op1 is None:
                g.op(eng, lambda e: e.tensor_scalar(out, in0, s1, None, op0), reads=r, writes=w)
            else:
                g.op(eng, lambda e: e.tensor_scalar(out, in0, s1, s2, op0, op1), reads=r, writes=w)

        def tt(out, in0, in1, op, r=(), w=(), eng=DVE):
            g.op(eng, lambda e: e.tensor_tensor(out, in0, in1, op), reads=r, writes=w)

        def cp(out, in_, r=(), w=(), eng=DVE):
            g.op(eng, lambda e: e.tensor_copy(out, in_), reads=r, writes=w)

        def mm(out, lhsT, rhs, start, stop, r=(), w=()):
            g.op(PE, lambda e: e.matmul(out, lhsT, rhs, start=start, stop=stop), reads=r, writes=w)

        def dma(eng, out, in_, r=(), w=(), key=None):
            g.op(eng, lambda e: e.dma_start(out=out, in_=in_), reads=r, writes=w, dma=key)

        def sigm(dst, src, srckeys, dkey, nbias=0.0, extra=()):
            act(dst, src, AF.Exp, bias=nbias, scale=-1.0, r=list(srckeys) + list(extra), w=[dkey])
            act(dst, dst, AF.Ln, bias=ONE, r=[dkey, "cst"], w=[dkey])
            act(dst, dst, AF.Exp, scale=-1.0, r=[dkey], w=[dkey])

        def rsqrt_act(dst, src, srckeys, dkey, bias=0.0):
            act(dst, src, AF.Ln, bias=bias, r=list(srckeys) + ["cst"], w=[dkey])
            act(dst, dst, AF.Exp, scale=-0.5, r=[dkey], w=[dkey])

        wcount = [0]

        wmode = ["normal"]

        def load_w(src2d, tag=None):
            i = wcount[0] % NB
            wcount[0] += 1
            if wmode[0] == "cached" and tag is not None:
                dma(SP, wb[i][:].rearrange("p k n -> p (k n)"), wcache[tag[0], tag[1]], r=[("wc", tag)],
                    w=[("wb", i)], key=("w", i))
                return wb[i], ("wb", i)
            dma(POOL, wb[i][:], src2d.rearrange("(k p) n -> p k n", p=128), w=[("wb", i)], key=("w", i))
            if wmode[0] == "store" and tag is not None:
                dma(SP, wcache[tag[0], tag[1]], wb[i][:].rearrange("p k n -> p (k n)"), r=[("wb", i)],
                    w=[("wc", tag)], key=("wcs", i))
            return wb[i], ("wb", i)

        def load_layer_params(l):
            dma(POOL, wblk[:], wblk_d[l], w=["wblk"], key="wblk")
            dma(POOL, wsT[:], wsT_d[l].rearrange("h s t -> s h t"), w=["wsT"], key="wsT")
            g.op(POOL, lambda e: e.affine_select(out=wsT[:], in_=wsT[:], pattern=[[0, 8], [1, 128]],
                                                 compare_op=ALU.is_ge, fill=0.0, base=0, channel_multiplier=-1),
                 reads=["wsT"], writes=["wsT"])
            dma(SP, bsT[:], bsx_d[l], w=["bsT"], key="bsT")
            dma(SP, lncb[:].rearrange("p q n -> p (q n)"), lnc_d[l].to_broadcast([128, 1024]), w=["lncb"], key="lncb")

        dma(SP, PV[:], pvec, w=["PV"], key="PV")
        dma(SP, FG[:], fgv, w=["FG"], key="FG")
        dma(SP, ct, cT.rearrange("p k s -> p (k s)"), w=[("rg", 0)], key="ct")
        g.op(DVE, lambda e: e.memset(ident[:], 1.0), writes=["ident"])
        g.op(POOL, lambda e: e.affine_select(out=ident[:], in_=ident[:], pattern=[[1, 128]], compare_op=ALU.is_equal,
                                             fill=0.0, base=0, channel_multiplier=-1), reads=["ident"], writes=["ident"])
        g.op(DVE, lambda e: e.memset(ones_d[:], 1.0 / D), writes=["ones"])
        g.op(DVE, lambda e: e.memset(ones_b[:], 1.0 / 512), writes=["ones"])
        g.op(DVE, lambda e: e.memset(hist_a[:], 0.0), writes=["hist_a"])
        g.op(DVE, lambda e: e.memset(hst[:], 0.0), writes=["hst"])
        g.op(DVE, lambda e: e.memset(hist_b[:], 0.0), writes=["hist_b"])
        g.op(DVE, lambda e: e.memset(cst[:, 0:1], EPS), writes=["cst"])
        g.op(DVE, lambda e: e.memset(cst[:, 1:2], 1.0), writes=["cst"])
        for l in range(L):
            ts(PV2[:, l, D_NBA:D_NBA + 8], PV[:, l, O_BA:O_BA + 8], -1.0, None, ALU.mult, r=["PV"], w=["PV2"])
            ts(PV2[:, l, D_NBI:D_NBI + 8], PV[:, l, O_BI:O_BI + 8], -1.0, None, ALU.mult, r=["PV"], w=["PV2"])
            sm1, k1 = small.get()
            act(sm1[:, 0:8], PV[:, l, O_LAM:O_LAM + 8], AF.Exp, scale=-1.0, r=["PV"], w=[k1])
            act(sm1[:, 0:8], sm1[:, 0:8], AF.Ln, bias=ONE, r=[k1, "cst"], w=[k1])
            ts(PV2[:, l, D_C:D_C + 8], sm1[:, 0:8], -8.0, None, ALU.mult, r=[k1], w=["PV2"])
            ts(PV2[:, l, D_C2:D_C2 + 8], sm1[:, 0:8], -16.0, None, ALU.mult, r=[k1], w=["PV2"])
            ts(PV2[:, l, D_CH:D_CH + 8], sm1[:, 0:8], -4.0, None, ALU.mult, r=[k1], w=["PV2"])
            ts(PV2[:, l, D_HBA:D_HBA + 8], PV[:, l, O_BA:O_BA + 8], 0.5, None, ALU.mult, r=["PV"], w=["PV2"])
            ts(PV2[:, l, D_HBI:D_HBI + 8], PV[:, l, O_BI:O_BI + 8], 0.5, None, ALU.mult, r=["PV"], w=["PV2"])

        sigm(cth, ct, [("rg", 0)], ("rg", 1))
        tt(scb[:].rearrange("p k s -> p (k s)"), cth, ct, ALU.mult, r=[("rg", 0), ("rg", 1)], w=["scb"])
        NG_ADA = 3 * D // WG
        CPG = WG // 128
        for l in range(L):
            for gi in range(NG_ADA):
                wt, wk = load_w(w_ada[l][:, gi * WG:(gi + 1) * WG])
                pb = psB[gi % 4]
                pk = ("psB", gi % 4)
                for m4 in range(CPG):
                    for k in range(KC):
                        mm(pb[:, m4 * 32:m4 * 32 + NS], wt[:, k, m4 * 128:(m4 + 1) * 128], scb[:, k, :],
                           k == 0, k == KC - 1, r=[wk, "scb"], w=[pk])
                tt(modT[:, l, gi * CPG:(gi + 1) * CPG, :],
                   pb[:, 0:32 * CPG].rearrange("p (a b) -> p a b", b=32)[:, :, 0:NS],
                   PV[:, l, O_BADA + gi * CPG:O_BADA + (gi + 1) * CPG].unsqueeze(2).to_broadcast([128, CPG, NS]),
                   ALU.add, r=[pk, "PV"], w=["mod"])
            ts(modT[:, l, 16:32, :], modT[:, l, 16:32, :], 1.0, None, ALU.add, r=["mod"], w=["mod"])
            tt(modT[:, l, 16:32, :], modT[:, l, 16:32, :],
               PV[:, l, O_NG:O_NG + 16].unsqueeze(2).to_broadcast([128, 16, NS]), ALU.mult, r=["mod", "PV"], w=["mod"])

        class Cx:
            pass

        PCX = Cx()
        PCX.N, PCX.nseq, PCX.tps, PCX.s0, PCX.sample = T, 1, T, 0, False
        SCX = Cx()
        SCX.N, SCX.nseq, SCX.tps, SCX.s0, SCX.sample = NSQ * ST, NSQ, ST, 1, True

        class Pipe:
            def __init__(self):
                self.items = []

            def add(self, stages):
                self.items.append(stages)

            def run(self):
                n = len(self.items)
                if n == 0:
                    return
                maxlag = max(lg for it in self.items for lg, _ in it)
                for t in range(n + maxlag):
                    todo = []
                    for i in range(max(0, t - maxlag), min(n, t + 1)):
                        for lg, fn in self.items[i]:
                            if i + lg == t:
                                todo.append((-lg, i, fn))
                    todo.sort(key=lambda x: (x[0], x[1]))
                    for _, _, fn in todo:
                        fn()
                self.items = []

        zcnt = [0]

        def rms_sq(cx, k, first, last):
            N = cx.N
            sq, sk = ringb.get()
            act(sq[:, :N], xT[:, k, :N], AF.Square, r=[("x", k)], w=[sk])
            mm(psB[0][:, :N], ones_d[:], sq[:, :N], first, last, r=[sk, "ones"], w=[("psB", 0)])

        def rms_fin(cx):
            rsqrt_act(rstd[:, :cx.N], psB[0][:, :cx.N], [("psB", 0)], "rstd", bias=EPSC)

        def run_layer(cx, l):
            N, nseq, tps, s0 = cx.N, cx.nseq, cx.tps, cx.s0
            pvs = lambda off: PV[:, l, off:off + 1]
            pv2 = lambda off: PV2[:, l, off:off + 1]
            pipe = Pipe()

            def V(ap):
                return ap.rearrange("p (s t) -> p s t", t=tps)

            def bcs(idx):
                return modT[:, l, idx, s0:s0 + nseq].unsqueeze(2).to_broadcast([128, nseq, tps])

            load_layer_params(l)
            if cx.sample:
                dma(POOL, BD[:], ws8_d[l], w=["BD"], key="BD")
                tt(BD[:], BD[:], mask8[:].unsqueeze(1).to_broadcast([128, 8, 128]), ALU.mult,
                   r=["BD", "mask8"], w=["BD"])
                dma(SP, sst_a[:], slc[:, l], w=["sst_a"], key="sst_a")
                dma(SP, sst_h[:], slh[:, l], w=["sst_h"], key="sst_h")
                dma(SP, sst_c[:], scc[:, l], w=["sst_c"], key="sst_c")
                ha = lambda j: sst_a[:, j, :, :]
                h0 = lambda j: sst_h[:, j, :]
                hb = lambda jj: sst_c[:, jj, :, :]
                hak, h0k, hbk = "sst_a", "sst_h", "sst_c"
            else:
                ha = lambda j: hist_a[:, l, j, :].unsqueeze(1)
                h0 = lambda j: hst[:, l, j:j + 1]
                hb = lambda jj: hist_b[:, l, jj, :].unsqueeze(1)
                hak, h0k, hbk = "hist_a", "hst", "hist_b"

            for k in range(KC):
                t, tk = ring.get()
                if cx.sample:
                    tt(t[:, :N], xT[:, k, :N], rstd[:, :N], ALU.mult, r=[("x", k), "rstd"], w=[tk])
                    tt(V(t[:, :N]), V(t[:, :N]), bcs(16 + k), ALU.mult, r=[tk, "mod"], w=[tk])
                    tt(V(xn[:, k, :N]), V(t[:, :N]), bcs(k), ALU.add, r=[tk, "mod"], w=[("xn", k)])
                else:
                    stt(t[:, :N], xT[:, k, :N], modT[:, l, 16 + k, 0:1], rstd[:, :N], ALU.mult, ALU.mult,
                        r=[("x", k), "rstd", "mod"], w=[tk])
                    act(xn[:, k, :N], t[:, :N], AF.Identity, bias=modT[:, l, k, 0:1], r=[tk, "mod"],
                        w=[("xn", k)])
            wl = w_in[l]
            itemno = [0]

            def zitems(name, gidx, mk_stages):
                c0 = IN_OFF[name] + gidx * WG
                box = {}

                def ld():
                    box["w"] = load_w(wl[:, c0:c0 + WG], tag=(l, c0 // WG))

                for jl in range(CPG):
                    S = Cx()
                    S.c = gidx * CPG + jl
                    S.par = itemno[0] % 2
                    itemno[0] += 1

                    def zfn(S=S, jl=jl, first=(jl == 0)):
                        if first:
                            ld()
                        wt, wk = box["w"]
                        b = zcnt[0] % 4
                        zcnt[0] += 1
                        S.zp, S.zk = psA[b][:, :N], ("psA", b)
                        for k in range(KC):
                            mm(S.zp, wt[:, k, jl * 128:(jl + 1) * 128], xn[:, k, :N], k == 0, k == KC - 1,
                               r=[wk, ("xn", k)], w=[S.zk])
                    pipe.add([(0, zfn)] + mk_stages(S))

            def sig_den(dst, src, skeys, dkey, nbias=0.0, extra=()):
                act(dst, src, AF.Exp, bias=nbias, scale=-1.0, r=list(skeys) + list(extra), w=[dkey])
                ts(dst, dst, 1.0, None, ALU.add, r=[dkey], w=[dkey])

            def st_xa(S):
                j = S.c
                jj = j % 4
                gb_ = (psB[1], psB[2]) if S.par == 0 else (psB[0], psB[3])
                gk_ = (("psB", 1), ("psB", 2)) if S.par == 0 else (("psB", 0), ("psB", 3))

                def front():
                    xet, xk = xa_ext.get()
                    xe = xet[:, 0:nseq * (3 + tps)].rearrange("p (s t) -> p s t", t=3 + tps)
                    cp(xe[:, :, 0:3], ha(j), r=[hak], w=[xk])
                    cp(xe[:, :, 3:3 + tps], V(S.zp), r=[S.zk], w=[xk])
                    cp(ha(j), xe[:, :, tps:tps + 3], r=[xk], w=[hak])
                    xct, S.xck = xcr.get()
                    S.xc = xct[:, :N]
                    ts(V(S.xc), xe[:, :, 0:tps], pvs(O_CAW + j * 4), pvs(O_CAB + j), ALU.mult, ALU.add,
                       r=[xk, "PV"], w=[S.xck])
                    for kk in range(1, 4):
                        stt(V(S.xc), xe[:, :, kk:kk + tps], pvs(O_CAW + j * 4 + kk), V(S.xc), ALU.mult, ALU.add,
                            r=[xk, S.xck, "PV"], w=[S.xck])
                    xcbt, S.xcbk = ringb.get()
                    S.xcb = xcbt[:, :N]
                    cp(S.xcb, S.xc, r=[S.xck], w=[S.xcbk])

                def gates():
                    mm(gb_[0][:, :N], wblk[:, 0, j, :], S.xcb, True, True, r=["wblk", S.xcbk], w=[gk_[0]])
                    mm(gb_[1][:, :N], wblk[:, 1, j, :], S.xcb, True, True, r=["wblk", S.xcbk], w=[gk_[1]])

                def back():
                    trt, trk = ring.get()
                    tr = trt[:, :N]
                    act(tr, gb_[0][:, :N], AF.Tanh, bias=pv2(D_HBA + j), scale=0.5, r=[gk_[0], "PV2"], w=[trk])
                    tit, tik = ring.get()
                    ti = tit[:, :N]
                    act(ti, gb_[1][:, :N], AF.Tanh, bias=pv2(D_HBI + j), scale=0.5, r=[gk_[1], "PV2"], w=[tik])
                    at, ak = ring.get()
                    a = at[:, :N]
                    act(a, tr, AF.Exp, bias=pv2(D_CH + j), scale=pv2(D_CH + j), r=[trk, "PV2"], w=[ak])
                    act(tr, tr, AF.Exp, bias=pv2(D_C + j), scale=pv2(D_C + j), r=[trk, "PV2"], w=[trk])
                    act(tr, tr, AF.Ln, bias=ONE, scale=-1.0, r=[trk, "cst"], w=[trk])
                    act(tr, tr, AF.Exp, scale=0.5, r=[trk], w=[trk])
                    stt(ti, ti, 1.0, S.xc, ALU.add, ALU.mult, r=[tik, S.xck], w=[tik])
                    stt(ti, ti, 0.5, tr, ALU.mult, ALU.mult, r=[tik, trk], w=[tik])
                    if nseq == 1:
                        g.op(DVE, lambda e: e.tensor_tensor_scan(
                            hold4[:, jj, :N], a, ti, h0(j)[:, 0:1], ALU.mult, ALU.add),
                            reads=[ak, tik, h0k], writes=[("hold4", jj)])
                    else:
                        smt, smk = small.get()
                        tt(smt[:, 0:nseq].unsqueeze(2), V(a)[:, :, 0:1], h0(j).unsqueeze(2), ALU.mult,
                           r=[ak, h0k], w=[smk])
                        tt(V(ti)[:, :, 0:1], V(ti)[:, :, 0:1], smt[:, 0:nseq].unsqueeze(2), ALU.add,
                           r=[tik, smk], w=[tik])
                        g.op(DVE, lambda e: e.memset(V(a)[:, :, 0:1], 0.0), reads=[smk], writes=[ak])
                        g.op(DVE, lambda e: e.tensor_tensor_scan(hold4[:, jj, :N], a, ti, 0.0, ALU.mult, ALU.add),
                             reads=[ak, tik], writes=[("hold4", jj)])
                    cp(h0(j), V(hold4[:, jj, :N])[:, :, tps - 1], r=[("hold4", jj)], w=[h0k])

                return [(1, front), (2, gates), (3, back)]

            def st_ga(S):
                j = S.c
                jj = j % 4

                def post():
                    tgt, tgk = ring.get()
                    tg = tgt[:, :N]
                    act(tg, S.zp, AF.Tanh, scale=0.5, r=[S.zk], w=[tgk])
                    stt(tg, tg, 1.0, S.zp, ALU.add, ALU.mult, r=[tgk, S.zk], w=[tgk])
                    stt(yb[:, j, :N], tg, 0.5, hold4[:, jj, :N], ALU.mult, ALU.mult,
                        r=[tgk, ("hold4", jj)], w=[("y", j)])
                return [(4, post)]

            def st_gla(S):
                def post():
                    act(hold4[:, S.c, :N], S.zp, AF.Copy, r=[S.zk], w=[("hold4", S.c)])
                return [(1, post)]

            def st_glb(S):
                jj = S.c
                cb_ps = psB[1 + (jj % 2)]
                cbpk = ("psB", 1 + (jj % 2))

                def front():
                    tbt, tbk = ring.get()
                    tb = tbt[:, :N]
                    sigm(tb, S.zp, [S.zk], tbk)
                    get_, gk = glu_ext.get()
                    ge = get_[:, 0:nseq * (30 + tps)].rearrange("p (s t) -> p s t", t=30 + tps)
                    gbt, S.gbk = glu_bf.get()
                    S.geb = gbt[:, 0:nseq * (30 + tps)].rearrange("p (s t) -> p s t", t=30 + tps)
                    cp(ge[:, :, 0:30], hb(jj), r=[hbk], w=[gk])
                    tt(ge[:, :, 30:30 + tps], V(hold4[:, jj, :N]), V(tb), ALU.mult,
                       r=[tbk, ("hold4", jj)], w=[gk])
                    cp(hb(jj), ge[:, :, tps:tps + 30], r=[gk], w=[hbk])
                    cp(gbt[:, 0:nseq * (30 + tps)], get_[:, 0:nseq * (30 + tps)], r=[gk], w=[S.gbk])
                    S.dgt, S.dgk = dgw.get()
                    tt(S.dgt[:], ident[:].unsqueeze(1).to_broadcast([128, 31, 128]),
                       PV[:, l, O_CBW + jj * 31:O_CBW + jj * 31 + 31].unsqueeze(2).to_broadcast([128, 31, 128]),
                       ALU.mult, r=["ident", "PV"], w=[S.dgk])

                def conv():
                    for kk in range(31):
                        mm(V(cb_ps[:, :N]), S.dgt[:, kk, :], S.geb[:, :, kk:kk + tps], kk == 0, kk == 30,
                           r=[S.dgk, S.gbk], w=[cbpk])

                def post_conv():
                    cbk = ("cb4", jj)
                    act(cb4[:, jj, :N], cb_ps[:, :N], AF.Identity, bias=pvs(O_CBB + jj), r=[cbpk, "PV"], w=[cbk])
                    cbf, S.cbfk = ringb.get()
                    S.cbf = cbf
                    act(cbf[:, :N], cb4[:, jj, :N], AF.Copy, r=[cbk], w=[S.cbfk])
                    cbs, S.cbsk = ringb.get()
                    S.cbs = cbs
                    act(cbs[:, :N], cb4[:, jj, :N], AF.Square, r=[cbk], w=[S.cbsk])

                def stats():
                    mm(psB[0][:, :N], ones_b[:], S.cbf[:, :N], jj == 0, jj == 3, r=[S.cbfk, "ones"], w=[("psB", 0)])
                    mm(psB[3][:, :N], ones_b[:], S.cbs[:, :N], jj == 0, jj == 3, r=[S.cbsk, "ones"], w=[("psB", 3)])

                def fin():
                    m2t, m2k = ring.get()
                    m2 = m2t[:, :N]
                    act(m2, psB[0][:, :N], AF.Square, r=[("psB", 0)], w=[m2k])
                    act(meanB[:, :N], psB[0][:, :N], AF.Copy, r=[("psB", 0)], w=["meanB"])
                    tt(m2, psB[3][:, :N], m2, ALU.subtract, r=[("psB", 3), m2k], w=[m2k])
                    ts(m2, m2, 0.0, EPS, ALU.max, ALU.add, r=[m2k], w=[m2k])
                    S.m2, S.m2k = m2, m2k

                def fin_b():
                    rsqrt_act(rstdB[:, :N], S.m2, [S.m2k], "rstdB")

                st = [(1, front), (2, conv), (3, post_conv), (4, stats)]
                if jj == 3:
                    st += [(5, fin), (6, fin_b)]
                return st

            def st_gb(S):
                jj = S.c

                def pre():
                    S.tgbk = ("hold4", jj)
                    S.tgb = hold4[:, jj, :N]
                    sigm(S.tgb, S.zp, [S.zk], S.tgbk)
                    tt(S.tgb, S.tgb, S.zp, ALU.mult, r=[S.tgbk, S.zk], w=[S.tgbk])

                def post_a():
                    xct, S.xck = ring.get()
                    S.xc = xct[:, :N]
                    tt(S.xc, cb4[:, jj, :N], meanB[:, :N], ALU.subtract, r=[("cb4", jj), "meanB"], w=[S.xck])
                    tt(S.xc, S.xc, rstdB[:, :N], ALU.mult, r=[S.xck, "rstdB"], w=[S.xck])

                def post_b():
                    xc, xck = S.xc, S.xck
                    lnt, lnk = ring.get()
                    ln = lnt[:, :N]
                    act(ln, xc, AF.Identity, bias=pvs(O_LBB + jj), scale=pvs(O_LBG + jj), r=[xck, "PV"], w=[lnk])
                    sigm(xc, ln, [lnk], xck)
                    tt(ln, ln, xc, ALU.mult, r=[xck, lnk], w=[lnk])
                    tt(yb[:, 8 + jj, :N], ln, S.tgb, ALU.mult, r=[lnk, S.tgbk], w=[("y", 8 + jj)])
                return [(1, pre), (7, post_a), (8, post_b)]

            for g4 in range(4):
                zitems("xa", g4, st_xa)
                zitems("ga", g4, st_ga)
            for gx in range(2):
                zitems("gla", gx, st_gla)
            for gx in range(2):
                zitems("glb", gx, st_glb)
            for gx in range(2):
                zitems("gb", gx, st_gb)
            pipe.run()

            nq = N // 128
            for hv in range(2):
                c0 = IN_OFF["v"] + hv * WG
                wt, wk = load_w(wl[:, c0:c0 + WG], tag=(l, c0 // WG))
                for q in range(nq):
                    for k in range(KC):
                        mm(psA[q][:, hv * WG:(hv + 1) * WG], xn[:, k, q * 128:(q + 1) * 128], wt[:, k, :],
                           k == 0, k == KC - 1, r=[wk, ("xn", k)], w=[("psA", q)])
            zcnt[0] = nq

            def vnorm(q):
                zp, zk = psA[q], ("psA", q)
                s6, s6k = small.get()
                g.op(DVE, lambda e: e.bn_stats(s6[:, 0:6], zp[:]), reads=[zk], writes=[s6k])
                mv, mvk = small.get()
                g.op(DVE, lambda e: e.bn_aggr(mv[:, 0:2], s6[:, 0:6]), reads=[s6k], writes=[mvk])
                rsqrt_act(mv[:, 2:3], mv[:, 1:2], [mvk], mvk, bias=EPSC)
                vh, vhk = ring.get()
                ts(vh[:], zp[:], mv[:, 0:1], mv[:, 2:3], ALU.subtract, ALU.mult, r=[zk, mvk], w=[vhk])
                tt(vh[:], vh[:], lncb[:, 0, :], ALU.mult, r=[vhk, "lncb"], w=[vhk])
                if cx.sample:
                    tt(vh[:], vh[:], lncb[:, 1, :], ALU.add, r=[vhk, "lncb"], w=[vhk])
                    dma(SP, gvs[l], vh[:], r=[vhk], key=("yout", vhk))
                    act(vn[:, q, :], vh[:], AF.Copy, r=[vhk], w=[("vn", q)])
                else:
                    tt(vn[:, q, :], vh[:], lncb[:, 1, :], ALU.add, r=[vhk, "lncb"], w=[("vn", q)])

            def st_u(S):
                def post():
                    act(hold4[:, S.c, :N], S.zp, AF.Copy, r=[S.zk], w=[("hold4", S.c)])
                return [(1, post)]

            def st_gc(S):
                jj = S.c
                mb = psB[1 + (jj % 2)]
                mbk = ("psB", 1 + (jj % 2))

                def pre():
                    tgct, S.tgck = ring.get()
                    S.tgc = tgct[:, :N]
                    sigm(S.tgc, S.zp, [S.zk], S.tgck)
                    tt(S.tgc, S.tgc, S.zp, ALU.mult, r=[S.tgck, S.zk], w=[S.tgck])

                def mix():
                    for q in range(nq):
                        for h2 in range(2):
                            hd = 2 * jj + h2
                            wmix = BD[:, hd, :] if cx.sample else wsT[:, hd, :]
                            mm(mb[h2 * 64:(h2 + 1) * 64, q * 128:(q + 1) * 128],
                               vn[:, q, hd * 64:(hd + 1) * 64], wmix, True, True,
                               r=[("vn", q), "BD" if cx.sample else "wsT"], w=[mbk])

                def post_mix():
                    m1t, m1k = ring.get()
                    m1 = m1t[:, :N]
                    if cx.sample:
                        bias_bc = bsT[:, jj, 0:ST].unsqueeze(1).to_broadcast([128, NSQ, ST])
                        tt(V(m1), V(mb[:, :N]), bias_bc, ALU.add, r=[mbk, "bsT"], w=[m1k])
                    else:
                        tt(m1.rearrange("p (q t) -> p q t", t=128), mb[:, :N].rearrange("p (q t) -> p q t", t=128),
                           bsT[:, jj, :].unsqueeze(1).to_broadcast([128, nq, 128]), ALU.add,
                           r=[mbk, "bsT"], w=[m1k])
                    tt(m1, m1, hold4[:, jj, :N], ALU.mult, r=[m1k, ("hold4", jj)], w=[m1k])
                    tt(yb[:, 12 + jj, :N], m1, S.tgc, ALU.mult, r=[m1k, S.tgck], w=[("y", 12 + jj)])
                return [(1, pre), (2, mix), (3, post_mix)]

            for q in range(nq):
                vnorm(q)
            for gx in range(2):
                zitems("u", gx, st_u)
            for gx in range(2):
                zitems("gc", gx, st_gc)
            pipe.run()

            for og in range(D // WG):
                box = {}
                for m4 in range(CPG):
                    S = Cx()
                    S.m = og * CPG + m4

                    def zfn(S=S, m4=m4, og=og, box=box):
                        if m4 == 0:
                            box["w"] = load_w(w_out[l][:, og * WG:(og + 1) * WG], tag=(l, 20 + og))
                        wt, wk = box["w"]
                        b = zcnt[0] % 4
                        zcnt[0] += 1
                        S.zp, S.zk = psA[b][:, :N], ("psA", b)
                        for k in range(KC):
                            mm(S.zp, wt[:, k, m4 * 128:(m4 + 1) * 128], yb[:, k, :N], k == 0, k == KC - 1,
                               r=[wk, ("y", k)], w=[S.zk])

                    def resid(S=S):
                        m = S.m
                        if cx.sample:
                            ot, otk = ring.get()
                            tt(V(ot[:, :N]), V(S.zp), bcs(32 + m), ALU.mult, r=[S.zk, "mod"], w=[otk])
                            tt(xT[:, m, :N], xT[:, m, :N], ot[:, :N], ALU.add, r=[otk, ("x", m)], w=[("x", m)])
                        else:
                            stt(xT[:, m, :N], S.zp, modT[:, l, 32 + m, 0:1], xT[:, m, :N], ALU.mult, ALU.add,
                                r=[S.zk, ("x", m), "mod"], w=[("x", m)])
                        sq, S.sk = ringb.get()
                        S.sq = sq
                        act(sq[:, :N], xT[:, m, :N], AF.Square, r=[("x", m)], w=[S.sk])

                    def ssq(S=S):
                        mm(psB[0][:, :N], ones_d[:], S.sq[:, :N], S.m == 0, S.m == KC - 1,
                           r=[S.sk, "ones"], w=[("psB", 0)])
                    pipe.add([(0, zfn), (1, resid), (2, ssq)])
            pipe.run()
            rms_fin(cx)
            if cx.sample:
                dma(SP, lcs[:, l], sst_a[:], r=["sst_a"], key="sst_a")
                dma(SP, lhs[:, l], sst_h[:], r=["sst_h"], key="sst_h")
                dma(SP, ccs[:, l], sst_c[:], r=["sst_c"], key="sst_c")

        def run_tile(cx, load_fn, store_fn):
            load_fn()
            for k in range(KC):
                rms_sq(cx, k, k == 0, k == KC - 1)
            rms_fin(cx)
            for l in range(L):
                run_layer(cx, l)
            stg = [xn[:].rearrange("p k t -> p (k t)").bitcast(F32), yb[:].rearrange("p k t -> p (k t)").bitcast(F32)]
            for k in range(KC):
                buf = stg[k // 8]
                c = k % 8
                t = buf[:, c * T:c * T + cx.N]
                key = "xn" if k < 8 else "y"
                stt(t, xT[:, k, :cx.N], FG[:, k:k + 1], rstd[:, :cx.N], ALU.mult, ALU.mult,
                    r=[("x", k), "rstd", "FG"], w=[(key, 2 * c), (key, 2 * c + 1)])
                if k % 4 == 3:
                    c0 = c - 3
                    src = buf[:, c0 * T:(c0 + 4) * T].rearrange("p (k t) -> p k t", t=T)[:, :, :cx.N]
                    store_fn(k // 4, src, [(key, 2 * c0 + i) for i in range(8)])

        QS = (SP, ACT)
        for p in range(NTILE):
            wmode[0] = "store" if (with_sample and p == NTILE - 1) else "normal"
            run_tile(PCX,
                     lambda p=p: [dma(QS[g4 % 2], xT[:, 4 * g4:4 * g4 + 4, :],
                                      xpT[g4 * 512:(g4 + 1) * 512, p * T:(p + 1) * T].rearrange("(k p) t -> p k t", p=128),
                                      w=[("x", 4 * g4 + i) for i in range(4)], key=("xin", g4)) for g4 in range(4)],
                     lambda g4, src, keys, p=p: dma(QS[g4 % 2],
                                                    ypT[g4 * 512:(g4 + 1) * 512, p * T:(p + 1) * T].rearrange(
                                                        "(k p) t -> p k t", p=128),
                                                    src, r=keys, key=("yst", g4)))
        if with_sample:
            dma(POOL, mask8[:], mask8_d, w=["mask8"], key="mask8")
            wmode[0] = "cached"
            run_tile(SCX,
                     lambda: [dma(QS[g4 % 2], xT[:, 4 * g4:4 * g4 + 4, :SCX.N], xsT[:, 4 * g4:4 * g4 + 4, :],
                                  w=[("x", 4 * g4 + i) for i in range(4)], key=("xin", g4)) for g4 in range(4)],
                     lambda g4, src, keys: dma(QS[g4 % 2], ysT[:, 4 * g4:4 * g4 + 4, :], src, r=keys, key=("yst", g4)))

        dma(SP, lcp, hist_a[:], r=["hist_a"], key="sout")
        dma(SP, lhp, hst[:], r=["hst"], key="sout")
        dma(SP, ccp, hist_b[:], r=["hist_b"], key="sout")

        emit_graph(nc, g, st)
        build_program.sbuf_left = nc.sbuf_bytes_remaining
    return nc


_NC_CACHE = {}


def _pack_pvec(norm_g, b_ada, conv_a_w, conv_a_b, lru_ba, lru_bi, lru_lam, conv_b_w, conv_b_b, ln_b_g, ln_b_b):
    pv = np.zeros((128, L, NP), np.float32)
    fm = lambda v, nch: v.reshape(nch, 128).T
    for l in range(L):
        pv[:, l, O_NG:O_NG + 16] = fm(norm_g[l], 16)
        pv[:, l, O_BADA:O_BADA + 48] = fm(b_ada[l], 48)
        pv[:, l, O_CAW:O_CAW + 32] = conv_a_w[l].reshape(4, 8, 128).transpose(2, 1, 0).reshape(128, 32)
        pv[:, l, O_CAB:O_CAB + 8] = fm(conv_a_b[l], 8)
        pv[:, l, O_BA:O_BA + 8] = fm(lru_ba[l], 8)
        pv[:, l, O_BI:O_BI + 8] = fm(lru_bi[l], 8)
        pv[:, l, O_LAM:O_LAM + 8] = fm(lru_lam[l], 8)
        pv[:, l, O_CBW:O_CBW + 124] = conv_b_w[l].reshape(31, 4, 128).transpose(2, 1, 0).reshape(128, 124)
        pv[:, l, O_CBB:O_CBB + 4] = fm(conv_b_b[l], 4)
        pv[:, l, O_LBG:O_LBG + 4] = fm(ln_b_g[l], 4)
        pv[:, l, O_LBB:O_LBB + 4] = fm(ln_b_b[l], 4)
    return pv


def _pack_wblk(wa, wi):
    out = np.zeros((L, 128, 2, 8, 128), np.float32)
    for which, w in enumerate((wa, wi)):
        for j in range(8):
            for half in range(2):
                out[:, half * 64:(half + 1) * 64, which, j, half * 64:(half + 1) * 64] = w[:, 2 * j + half]
    return out


def _pack_bsx(bs):
    out = np.zeros((L, 128, 4, 128), np.float32)
    for jj in range(4):
        for half in range(2):
            out[:, half * 64:(half + 1) * 64, jj, :] = bs[:, 2 * jj + half][:, None, :]
    return out


def _pack_ws8x(ws):
    out = np.zeros((L, 128, 8, 128), np.float32)
    blk = ws[:, :, :ST, :ST].transpose(0, 3, 1, 2)
    for sq in range(NSQ):
        out[:, sq * ST:(sq + 1) * ST, :, sq * ST:(sq + 1) * ST] = blk
    return out


def _mask8():
    m = np.zeros((128, 128), np.float32)
    for sq in range(NSQ):
        for tp in range(ST):
            m[sq * ST + tp, sq * ST + tp:(sq + 1) * ST] = 1.0
    return m


def kernel(x_prompt, x_sample, c_prompt, c_sample, state_lru_conv, state_lru_h, state_ccm_conv,
           norm_g, w_ada, b_ada, w_in, conv_a_w, conv_a_b, lru_wa, lru_ba, lru_wi, lru_bi, lru_lam,
           conv_b_w, conv_b_b, ln_b_g, ln_b_b, ln_c_g, ln_c_b, gmlp_ws, gmlp_bs, w_out, final_g):
    f = lambda a: np.ascontiguousarray(np.asarray(a, dtype=np.float32))
    x_prompt, x_sample, c_prompt, c_sample = f(x_prompt), f(x_sample), f(c_prompt), f(c_sample)
    state_lru_conv, state_lru_h, state_ccm_conv = f(state_lru_conv), f(state_lru_h), f(state_ccm_conv)
    with_sample = True
    if "nc" not in _NC_CACHE:
        _NC_CACHE["nc"] = build_program(with_sample)
    nc = _NC_CACHE["nc"]
    pv = _pack_pvec(*[np.asarray(a, np.float32) for a in
                      (norm_g, b_ada, conv_a_w, conv_a_b, lru_ba, lru_bi, lru_lam, conv_b_w, conv_b_b, ln_b_g, ln_b_b)])
    fg = f(np.asarray(final_g, np.float32).reshape(16, 128).T)
    shared = {
        "pvec": pv, "fgv": fg, "w_ada": f(w_ada), "w_in": f(w_in), "w_out": f(w_out),
        "wblk": _pack_wblk(np.asarray(lru_wa, np.float32), np.asarray(lru_wi, np.float32)),
        "wsT": f(np.asarray(gmlp_ws, np.float32).transpose(0, 1, 3, 2)),
        "bsx": _pack_bsx(np.asarray(gmlp_bs, np.float32)),
        "ws8x": _pack_ws8x(np.asarray(gmlp_ws, np.float32)), "mask8": _mask8(),
        "lnc": f(np.concatenate([np.asarray(ln_c_g), np.asarray(ln_c_b)], axis=1).reshape(L, 1, 1024)),
    }
    in_maps = []
    for c in range(8):
        b = c % 4
        cs = np.concatenate([c_prompt[b:b + 1], c_sample[NSQ * c:NSQ * (c + 1)]], axis=0)
        m = dict(shared)
        m["xpT"] = f(x_prompt[b].T)
        m["cT"] = f(cs.T.reshape(KC, 128, NS).transpose(1, 0, 2))
        sl = slice(NSQ * c, NSQ * (c + 1))
        m["xsT"] = f(x_sample[sl].reshape(NSQ * ST, KC, 128).transpose(2, 1, 0))
        m["slc"] = f(state_lru_conv[:, sl].reshape(L, NSQ, 3, 8, 128).transpose(4, 0, 3, 1, 2))
        m["slh"] = f(state_lru_h[:, sl].reshape(L, NSQ, 8, 128).transpose(3, 0, 2, 1))
        m["scc"] = f(state_ccm_conv[:, sl].reshape(L, NSQ, 30, 4, 128).transpose(4, 0, 3, 1, 2))
        in_maps.append(m)
    res = run_bass_kernel_spmd(nc, in_maps, core_ids=list(range(8)))
    R = res.results
    B = 4
    y_prompt = np.stack([R[b]["ypT"].T for b in range(B)])
    lc_p = np.stack([R[b]["lcp"].transpose(1, 3, 2, 0).reshape(L, 3, 1024) for b in range(B)], axis=1)
    lh_p = np.stack([R[b]["lhp"].transpose(1, 2, 0).reshape(L, 1024) for b in range(B)], axis=1)
    cc_p = np.stack([R[b]["ccp"].transpose(1, 3, 2, 0).reshape(L, 30, 512) for b in range(B)], axis=1)
    y_sample = np.concatenate([R[c]["ysT"].transpose(2, 1, 0).reshape(NSQ, ST, D) for c in range(8)], axis=0)
    lc_s = np.concatenate([R[c]["lcs"].transpose(1, 3, 4, 2, 0).reshape(L, NSQ, 3, 1024) for c in range(8)], axis=1)
    lh_s = np.concatenate([R[c]["lhs"].transpose(1, 3, 2, 0).reshape(L, NSQ, 1024) for c in range(8)], axis=1)
    cc_s = np.concatenate([R[c]["ccs"].transpose(1, 3, 4, 2, 0).reshape(L, NSQ, 30, 512) for c in range(8)], axis=1)
    gv_s = np.concatenate([R[c]["gvs"].reshape(L, NSQ, ST, 512) for c in range(8)], axis=1)
    y_sample, lc_s, lh_s, cc_s, gv_s = [np.ascontiguousarray(a) for a in (y_sample, lc_s, lh_s, cc_s, gv_s)]
    return (np.ascontiguousarray(y_prompt), y_sample, np.ascontiguousarray(lc_p), np.ascontiguousarray(lh_p),
            np.ascontiguousarray(cc_p), lc_s, lh_s, cc_s, gv_s)
```

```python
import contextlib
import numpy as np
import concourse.bass as bass
import concourse.mybir as mybir
from concourse.bass_utils import run_bass_kernel_spmd

F32 = mybir.dt.float32
BF16 = mybir.dt.bfloat16
ALU = mybir.AluOpType
AF = mybir.ActivationFunctionType
PE, ACT, DVE, POOL, SP = "tensor", "scalar", "vector", "gpsimd", "sync"

L = 4
D = 2048
KC = 16
T = 512
NTILE = 4
SEQ = 2048
NSQ = 16
ST = 8
NS = 1 + NSQ
EPS = 1e-6
NB = 3
WG = 256

O_NG, O_BADA, O_CAW, O_CAB, O_BA, O_BI, O_LAM, O_CBW, O_CBB, O_LBG, O_LBB, NP = \
    0, 16, 64, 96, 104, 112, 120, 128, 252, 256, 260, 264
D_NBA, D_NBI, D_C, D_C2, D_HBA, D_HBI, D_CH, ND = 0, 8, 16, 24, 32, 40, 48, 56

IN_OFF = {"xa": 0, "ga": 1024, "gla": 2048, "glb": 2560, "gb": 3072, "u": 3584, "v": 4096, "gc": 4608}


class Op:
    __slots__ = ("eng", "fn", "deps", "is_dma", "sem", "val", "signal", "idx", "dkey")

    def __init__(self, eng, fn, is_dma, dkey):
        self.eng = eng
        self.fn = fn
        self.deps = []
        self.is_dma = is_dma
        self.sem = None
        self.val = 0
        self.signal = False
        self.dkey = dkey


class Graph:
    def __init__(self):
        self.ops = []
        self.last_w = {}
        self.readers = {}

    def op(self, eng, fn, reads=(), writes=(), dma=None):
        o = Op(eng, fn, dma is not None, dma)
        o.idx = len(self.ops)
        deps = {}
        for k in reads:
            w = self.last_w.get(k)
            if w is not None:
                deps[w.idx] = (w, True)
        for k in writes:
            w = self.last_w.get(k)
            if w is not None and w.idx not in deps:
                deps[w.idx] = (w, False)
            for r in self.readers.get(k, ()):
                if r.idx not in deps:
                    deps[r.idx] = (r, False)
        for w, raw in deps.values():
            if (not w.is_dma) and (not o.is_dma) and w.eng == o.eng:
                if o.eng == PE:
                    continue
                if (not raw) and o.eng == POOL:
                    continue
            o.deps.append(w)
            w.signal = True
        for k in reads:
            self.readers.setdefault(k, []).append(o)
        for k in writes:
            self.last_w[k] = o
            self.readers[k] = []
        self.ops.append(o)
        return o

    def finalize(self):
        for o in self.ops:
            best = {}
            for d in o.deps:
                k = ("dma", d.dkey) if d.is_dma else ("eng", d.eng)
                if k not in best or d.idx > best[k].idx:
                    best[k] = d
            o.deps = list(best.values())
        for o in self.ops:
            o.signal = o.is_dma
        for o in self.ops:
            for d in o.deps:
                d.signal = True
        eng_cnt = {}
        dma_cnt = {}
        for o in self.ops:
            if o.is_dma:
                o.signal = True
                c = dma_cnt.get(o.dkey, 0) + 16
                dma_cnt[o.dkey] = c
                o.sem = ("dma", o.dkey)
                o.val = c
            elif o.signal:
                c = eng_cnt.get(o.eng, 0) + 1
                eng_cnt[o.eng] = c
                o.sem = ("eng", o.eng)
                o.val = c
        per_eng = {}
        for o in self.ops:
            per_eng.setdefault(o.eng, []).append(o)
        return per_eng, dma_cnt


def emit_graph(nc, g, st):
    per_eng, dma_cnt = g.finalize()
    keys = sorted({o.sem for o in g.ops if o.signal}, key=str)
    sems = {k: st.enter_context(nc.semaphore(f"sm{i}")) for i, k in enumerate(keys)}
    block = st.enter_context(nc.Block())
    finals = [(("dma", k), v) for k, v in dma_cnt.items()]

    def mk(engname, ops):
        def body(e):
            waited = {}
            for o in ops:
                for d in o.deps:
                    if waited.get(d.sem, 0) >= d.val:
                        continue
                    e.wait_ge(sems[d.sem], d.val)
                    waited[d.sem] = d.val
                ins = o.fn(e)
                if o.signal:
                    ins.then_inc(sems[o.sem], 16 if o.is_dma else 1)
            if engname == SP:
                for sem, val in finals:
                    if waited.get(sem, 0) < val:
                        e.wait_ge(sems[sem], val)
        return body

    for engname in (SP, ACT, DVE, POOL, PE):
        ops = per_eng.get(engname, [])
        if not ops and engname != SP:
            continue
        getattr(block, engname)(mk(engname, ops))


class Ring:
    def __init__(self, tiles, name):
        self.tiles = tiles
        self.name = name
        self.i = 0

    def get(self):
        i = self.i % len(self.tiles)
        self.i += 1
        return self.tiles[i], (self.name, i)


def build_program(with_sample=True):
    nc = bass.Bass("TRN2", target_bir_lowering=False)
    g = Graph()

    def din(name, shape):
        return nc.dram_tensor(name, list(shape), F32, kind="ExternalInput").ap()

    def dout(name, shape):
        return nc.dram_tensor(name, list(shape), F32, kind="ExternalOutput").ap()

    xpT = din("xpT", [D, SEQ])
    cT = din("cT", [128, KC, NS])
    pvec = din("pvec", [128, L, NP])
    fgv = din("fgv", [128, KC])
    w_ada = din("w_ada", [L, D, 3 * D])
    w_in = din("w_in", [L, D, 5120])
    w_out = din("w_out", [L, D, D])
    wblk_d = din("wblk", [L, 128, 2, 8, 128])
    wsT_d = din("wsT", [L, 8, 128, 128])
    bsx_d = din("bsx", [L, 128, 4, 128])
    lnc_d = din("lnc", [L, 1, 1024])
    wcache = nc.dram_tensor("wcache", [L, 28, 128, KC * WG], BF16, kind="Internal").ap()
    ypT = dout("ypT", [D, SEQ])
    lcp = dout("lcp", [128, L, 8, 3])
    lhp = dout("lhp", [128, L, 8])
    ccp = dout("ccp", [128, L, 4, 30])
    if with_sample:
        xsT = din("xsT", [128, KC, NSQ * ST])
        slc = din("slc", [128, L, 8, NSQ, 3])
        slh = din("slh", [128, L, 8, NSQ])
        scc = din("scc", [128, L, 4, NSQ, 30])
        ysT = dout("ysT", [128, KC, NSQ * ST])
        lcs = dout("lcs", [128, L, 8, NSQ, 3])
        lhs = dout("lhs", [128, L, 8, NSQ])
        ccs = dout("ccs", [128, L, 4, NSQ, 30])
        gvs = dout("gvs", [L, NSQ * ST, 512])
        ws8_d = din("ws8x", [L, 128, 8, 128])
        mask8_d = din("mask8", [128, 128])

    with contextlib.ExitStack() as st:
        def sb(name, shape, dt=F32):
            return st.enter_context(nc.sbuf_tensor(name, list(shape), dt))

        def ps(name, shape, dt=F32):
            return st.enter_context(nc.psum_tensor(name, list(shape), dt))

        xT = sb("xT", [128, KC, T])
        xn = sb("xn", [128, KC, T], BF16)
        yb = sb("yb", [128, KC, T], BF16)
        wb = [sb(f"wb{i}", [128, KC, WG], BF16) for i in range(NB)]
        ring = Ring([sb(f"rg{i}", [128, T]) for i in range(6)], "rg")
        ringb = Ring([sb(f"rb{i}", [128, T], BF16) for i in range(4)], "rb")
        xcr = Ring([sb(f"xcr{i}", [128, T]) for i in range(3)], "xcr")
        hold4 = sb("hold4", [128, 4, T])
        cb4 = sb("cb4", [128, 4, T])
        xa_ext = Ring([sb(f"xae{i}", [128, T + 3]) for i in range(2)], "xae")
        GEW = max(T + 30, NSQ * (ST + 30))
        glu_ext = Ring([sb(f"gle{i}", [128, GEW]) for i in range(1)], "gle")
        glu_bf = Ring([sb(f"glb{i}", [128, GEW], BF16) for i in range(2)], "glb")
        dgw = Ring([sb(f"dgw{i}", [128, 31, 128], BF16) for i in range(2)], "dgw")
        ident = sb("ident", [128, 128], BF16)
        vn = sb("vn", [128, 4, 512], BF16)
        lncb = sb("lncb", [128, 2, 512])
        rstd = sb("rstd", [128, T])
        rstdB = sb("rstdB", [128, T])
        meanB = sb("meanB", [128, T])
        PV = sb("PV", [128, L, NP])
        PV2 = sb("PV2", [128, L, ND])
        FG = sb("FG", [128, KC])
        modT = sb("modT", [128, L, 48, NS])
        ct = ring.tiles[0][:, 0:KC * NS]
        cth = ring.tiles[1][:, 0:KC * NS]
        scb = sb("scb", [128, KC, NS], BF16)
        wblk = sb("wblk_s", [128, 2, 8, 128], BF16)
        wsT = sb("wsT_s", [128, 8, 128], BF16)
        bsT = sb("bsT", [128, 4, 128])
        ones_d = sb("ones_d", [128, 128], BF16)
        ones_b = sb("ones_b", [128, 128], BF16)
        hist_a = sb("hist_a", [128, L, 8, 3])
        hst = sb("hst", [128, L, 8])
        hist_b = sb("hist_b", [128, L, 4, 30])
        if with_sample:
            sst_a = sb("sst_a", [128, 8, NSQ, 3])
            sst_h = sb("sst_h", [128, 8, NSQ])
            sst_c = sb("sst_c", [128, 4, NSQ, 30])
            BD = sb("BD", [128, 8, 128], BF16)
            mask8 = sb("mask8_s", [128, 128], BF16)
        cst = sb("cst", [128, 2])
        EPSC = cst[:, 0:1]
        ONE = cst[:, 1:2]
        small = Ring([sb(f"sm{i}", [128, 16]) for i in range(8)], "sm")
        psA = [ps(f"psA{i}", [128, 512]) for i in range(4)]
        psB = [ps(f"psB{i}", [128, 512]) for i in range(4)]

        def act(out, in_, func, bias=0.0, scale=1.0, r=(), w=()):
            g.op(ACT, lambda e: e.activation(out, in_, func, bias=bias, scale=scale), reads=r, writes=w)

        def stt(out, in0, scalar, in1, op0, op1, r=(), w=(), eng=DVE):
            g.op(eng, lambda e: e.scalar_tensor_tensor(out, in0, scalar, in1, op0, op1), reads=r, writes=w)

        def ts(out, in0, s1, s2, op0, op1=None, r=(), w=(), eng=DVE):
            if op1 is None:
                g.op(eng, lambda e: e.tensor_scalar(out, in0, s1, None, op0), reads=r, writes=w)
            else:
                g.op(eng, lambda e: e.tensor_scalar(out, in0, s1, s2, op0, op1), reads=r, writes=w)

        def tt(out, in0, in1, op, r=(), w=(), eng=DVE):
            g.op(eng, lambda e: e.tensor_tensor(out, in0, in1, op), reads=r, writes=w)

        def cp(out, in_, r=(), w=(), eng=DVE):
            g.op(eng, lambda e: e.tensor_copy(out, in_), reads=r, writes=w)

        def mm(out, lhsT, rhs, start, stop, r=(), w=()):
            g.op(PE, lambda e: e.matmul(out, lhsT, rhs, start=start, stop=stop), reads=r, writes=w)

        def dma(eng, out, in_, r=(), w=(), key=None):
            g.op(eng, lambda e: e.dma_start(out=out, in_=in_), reads=r, writes=w, dma=key)

        def sigm(dst, src, srckeys, dkey, nbias=0.0, extra=()):
            act(dst, src, AF.Exp, bias=nbias, scale=-1.0, r=list(srckeys) + list(extra), w=[dkey])
            act(dst, dst, AF.Ln, bias=ONE, r=[dkey, "cst"], w=[dkey])
            act(dst, dst, AF.Exp, scale=-1.0, r=[dkey], w=[dkey])

        def rsqrt_act(dst, src, srckeys, dkey, bias=0.0):
            act(dst, src, AF.Ln, bias=bias, r=list(srckeys) + ["cst"], w=[dkey])
            act(dst, dst, AF.Exp, scale=-0.5, r=[dkey], w=[dkey])

        wcount = [0]

        wmode = ["normal"]

        def load_w(src2d, tag=None):
            i = wcount[0] % NB
            wcount[0] += 1
            if wmode[0] == "cached" and tag is not None:
                dma(SP, wb[i][:].rearrange("p k n -> p (k n)"), wcache[tag[0], tag[1]], r=[("wc", tag)],
                    w=[("wb", i)], key=("wch", i))
                return wb[i], ("wb", i)
            dma(POOL, wb[i][:], src2d.rearrange("(k p) n -> p k n", p=128), w=[("wb", i)], key=("w", i))
            if wmode[0] == "store" and tag is not None:
                dma(SP, wcache[tag[0], tag[1]], wb[i][:].rearrange("p k n -> p (k n)"), r=[("wb", i)],
                    w=[("wc", tag)], key=("wcs", i))
            return wb[i], ("wb", i)

        def load_layer_params(l):
            dma(POOL, wblk[:], wblk_d[l], w=["wblk"], key="wblk")
            dma(POOL, wsT[:], wsT_d[l].rearrange("h s t -> s h t"), w=["wsT"], key="wsT")
            g.op(POOL, lambda e: e.affine_select(out=wsT[:], in_=wsT[:], pattern=[[0, 8], [1, 128]],
                                                 compare_op=ALU.is_ge, fill=0.0, base=0, channel_multiplier=-1),
                 reads=["wsT"], writes=["wsT"])
            dma(SP, bsT[:], bsx_d[l], w=["bsT"], key="bsT")
            dma(SP, lncb[:].rearrange("p q n -> p (q n)"), lnc_d[l].to_broadcast([128, 1024]), w=["lncb"], key="lncb")

        dma(SP, PV[:], pvec, w=["PV"], key="PV")
        dma(SP, FG[:], fgv, w=["FG"], key="FG")
        dma(SP, ct, cT.rearrange("p k s -> p (k s)"), w=[("rg", 0)], key="ct")
        g.op(DVE, lambda e: e.memset(ident[:], 1.0), writes=["ident"])
        g.op(POOL, lambda e: e.affine_select(out=ident[:], in_=ident[:], pattern=[[1, 128]], compare_op=ALU.is_equal,
                                             fill=0.0, base=0, channel_multiplier=-1), reads=["ident"], writes=["ident"])
        g.op(DVE, lambda e: e.memset(ones_d[:], 1.0 / D), writes=["ones"])
        g.op(DVE, lambda e: e.memset(ones_b[:], 1.0 / 512), writes=["ones"])
        g.op(DVE, lambda e: e.memset(hist_a[:], 0.0), writes=["hist_a"])
        g.op(DVE, lambda e: e.memset(hst[:], 0.0), writes=["hst"])
        g.op(DVE, lambda e: e.memset(hist_b[:], 0.0), writes=["hist_b"])
        g.op(DVE, lambda e: e.memset(cst[:, 0:1], EPS), writes=["cst"])
        g.op(DVE, lambda e: e.memset(cst[:, 1:2], 1.0), writes=["cst"])
        for l in range(L):
            ts(PV2[:, l, D_NBA:D_NBA + 8], PV[:, l, O_BA:O_BA + 8], -1.0, None, ALU.mult, r=["PV"], w=["PV2"])
            ts(PV2[:, l, D_NBI:D_NBI + 8], PV[:, l, O_BI:O_BI + 8], -1.0, None, ALU.mult, r=["PV"], w=["PV2"])
            sm1, k1 = small.get()
            act(sm1[:, 0:8], PV[:, l, O_LAM:O_LAM + 8], AF.Exp, scale=-1.0, r=["PV"], w=[k1])
            act(sm1[:, 0:8], sm1[:, 0:8], AF.Ln, bias=ONE, r=[k1, "cst"], w=[k1])
            ts(PV2[:, l, D_C:D_C + 8], sm1[:, 0:8], -8.0, None, ALU.mult, r=[k1], w=["PV2"])
            ts(PV2[:, l, D_C2:D_C2 + 8], sm1[:, 0:8], -16.0, None, ALU.mult, r=[k1], w=["PV2"])
            ts(PV2[:, l, D_CH:D_CH + 8], sm1[:, 0:8], -4.0, None, ALU.mult, r=[k1], w=["PV2"])
            ts(PV2[:, l, D_HBA:D_HBA + 8], PV[:, l, O_BA:O_BA + 8], 0.5, None, ALU.mult, r=["PV"], w=["PV2"])
            ts(PV2[:, l, D_HBI:D_HBI + 8], PV[:, l, O_BI:O_BI + 8], 0.5, None, ALU.mult, r=["PV"], w=["PV2"])

        sigm(cth, ct, [("rg", 0)], ("rg", 1))
        tt(scb[:].rearrange("p k s -> p (k s)"), cth, ct, ALU.mult, r=[("rg", 0), ("rg", 1)], w=["scb"])
        NG_ADA = 3 * D // WG
        CPG = WG // 128
        for l in range(L):
            for gi in range(NG_ADA):
                wt, wk = load_w(w_ada[l][:, gi * WG:(gi + 1) * WG])
                pb = psB[gi % 4]
                pk = ("psB", gi % 4)
                for m4 in range(CPG):
                    for k in range(KC):
                        mm(pb[:, m4 * 32:m4 * 32 + NS], wt[:, k, m4 * 128:(m4 + 1) * 128], scb[:, k, :],
                           k == 0, k == KC - 1, r=[wk, "scb"], w=[pk])
                tt(modT[:, l, gi * CPG:(gi + 1) * CPG, :],
                   pb[:, 0:32 * CPG].rearrange("p (a b) -> p a b", b=32)[:, :, 0:NS],
                   PV[:, l, O_BADA + gi * CPG:O_BADA + (gi + 1) * CPG].unsqueeze(2).to_broadcast([128, CPG, NS]),
                   ALU.add, r=[pk, "PV"], w=["mod"])
            ts(modT[:, l, 16:32, :], modT[:, l, 16:32, :], 1.0, None, ALU.add, r=["mod"], w=["mod"])
            tt(modT[:, l, 16:32, :], modT[:, l, 16:32, :],
               PV[:, l, O_NG:O_NG + 16].unsqueeze(2).to_broadcast([128, 16, NS]), ALU.mult, r=["mod", "PV"], w=["mod"])

        class Cx:
            pass

        PCX = Cx()
        PCX.N, PCX.nseq, PCX.tps, PCX.s0, PCX.sample = T, 1, T, 0, False
        SCX = Cx()
        SCX.N, SCX.nseq, SCX.tps, SCX.s0, SCX.sample = NSQ * ST, NSQ, ST, 1, True

        class Pipe:
            def __init__(self):
                self.items = []

            def add(self, stages):
                self.items.append(stages)

            def run(self):
                n = len(self.items)
                if n == 0:
                    return
                maxlag = max(lg for it in self.items for lg, _ in it)
                for t in range(n + maxlag):
                    todo = []
                    for i in range(max(0, t - maxlag), min(n, t + 1)):
                        for lg, fn in self.items[i]:
                            if i + lg == t:
                                todo.append((-lg, i, fn))
                    todo.sort(key=lambda x: (x[0], x[1]))
                    for _, _, fn in todo:
                        fn()
                self.items = []

        zcnt = [0]

        def rms_sq(cx, k, first, last):
            N = cx.N
            sq, sk = ringb.get()
            act(sq[:, :N], xT[:, k, :N], AF.Square, r=[("x", k)], w=[sk])
            mm(psB[0][:, :N], ones_d[:], sq[:, :N], first, last, r=[sk, "ones"], w=[("psB", 0)])

        def rms_fin(cx):
            rsqrt_act(rstd[:, :cx.N], psB[0][:, :cx.N], [("psB", 0)], "rstd", bias=EPSC)

        def run_layer(cx, l):
            N, nseq, tps, s0 = cx.N, cx.nseq, cx.tps, cx.s0
            pvs = lambda off: PV[:, l, off:off + 1]
            pv2 = lambda off: PV2[:, l, off:off + 1]
            pipe = Pipe()

            def V(ap):
                return ap.rearrange("p (s t) -> p s t", t=tps)

            def bcs(idx):
                return modT[:, l, idx, s0:s0 + nseq].unsqueeze(2).to_broadcast([128, nseq, tps])

            load_layer_params(l)
            if cx.sample:
                dma(POOL, BD[:], ws8_d[l], w=["BD"], key="BD")
                tt(BD[:], BD[:], mask8[:].unsqueeze(1).to_broadcast([128, 8, 128]), ALU.mult,
                   r=["BD", "mask8"], w=["BD"])
                dma(SP, sst_a[:], slc[:, l], w=["sst_a"], key="sst_a")
                dma(SP, sst_h[:], slh[:, l], w=["sst_h"], key="sst_h")
                dma(SP, sst_c[:], scc[:, l], w=["sst_c"], key="sst_c")
                ha = lambda j: sst_a[:, j, :, :]
                h0 = lambda j: sst_h[:, j, :]
                hb = lambda jj: sst_c[:, jj, :, :]
                hak, h0k, hbk = "sst_a", "sst_h", "sst_c"
            else:
                ha = lambda j: hist_a[:, l, j, :].unsqueeze(1)
                h0 = lambda j: hst[:, l, j:j + 1]
                hb = lambda jj: hist_b[:, l, jj, :].unsqueeze(1)
                hak, h0k, hbk = "hist_a", "hst", "hist_b"

            for k in range(KC):
                t, tk = ring.get()
                if cx.sample:
                    tt(t[:, :N], xT[:, k, :N], rstd[:, :N], ALU.mult, r=[("x", k), "rstd"], w=[tk])
                    tt(V(t[:, :N]), V(t[:, :N]), bcs(16 + k), ALU.mult, r=[tk, "mod"], w=[tk])
                    tt(V(xn[:, k, :N]), V(t[:, :N]), bcs(k), ALU.add, r=[tk, "mod"], w=[("xn", k)])
                else:
                    stt(t[:, :N], xT[:, k, :N], modT[:, l, 16 + k, 0:1], rstd[:, :N], ALU.mult, ALU.mult,
                        r=[("x", k), "rstd", "mod"], w=[tk])
                    act(xn[:, k, :N], t[:, :N], AF.Identity, bias=modT[:, l, k, 0:1], r=[tk, "mod"],
                        w=[("xn", k)])
            wl = w_in[l]
            itemno = [0]

            def zitems(name, gidx, mk_stages):
                c0 = IN_OFF[name] + gidx * WG
                box = {}

                def ld():
                    box["w"] = load_w(wl[:, c0:c0 + WG], tag=(l, c0 // WG))

                for jl in range(CPG):
                    S = Cx()
                    S.c = gidx * CPG + jl
                    S.par = itemno[0] % 2
                    itemno[0] += 1

                    def zfn(S=S, jl=jl, first=(jl == 0)):
                        if first:
                            ld()
                        wt, wk = box["w"]
                        b = zcnt[0] % 4
                        zcnt[0] += 1
                        S.zp, S.zk = psA[b][:, :N], ("psA", b)
                        for k in range(KC):
                            mm(S.zp, wt[:, k, jl * 128:(jl + 1) * 128], xn[:, k, :N], k == 0, k == KC - 1,
                               r=[wk, ("xn", k)], w=[S.zk])
                    pipe.add([(0, zfn)] + mk_stages(S))

            def sig_den(dst, src, skeys, dkey, nbias=0.0, extra=()):
                act(dst, src, AF.Exp, bias=nbias, scale=-1.0, r=list(skeys) + list(extra), w=[dkey])
                ts(dst, dst, 1.0, None, ALU.add, r=[dkey], w=[dkey])

            def st_xa(S):
                j = S.c
                jj = j % 4
                gb_ = (psB[1], psB[2]) if S.par == 0 else (psB[0], psB[3])
                gk_ = (("psB", 1), ("psB", 2)) if S.par == 0 else (("psB", 0), ("psB", 3))

                def front():
                    xet, xk = xa_ext.get()
                    xe = xet[:, 0:nseq * (3 + tps)].rearrange("p (s t) -> p s t", t=3 + tps)
                    cp(xe[:, :, 0:3], ha(j), r=[hak], w=[xk])
                    act(xe[:, :, 3:3 + tps], V(S.zp), AF.Copy, r=[S.zk], w=[xk])
                    cp(ha(j), xe[:, :, tps:tps + 3], r=[xk], w=[hak])
                    xct, S.xck = xcr.get()
                    S.xc = xct[:, :N]
                    ts(V(S.xc), xe[:, :, 0:tps], pvs(O_CAW + j * 4), pvs(O_CAB + j), ALU.mult, ALU.add,
                       r=[xk, "PV"], w=[S.xck])
                    for kk in range(1, 4):
                        stt(V(S.xc), xe[:, :, kk:kk + tps], pvs(O_CAW + j * 4 + kk), V(S.xc), ALU.mult, ALU.add,
                            r=[xk, S.xck, "PV"], w=[S.xck])
                    xcbt, S.xcbk = ringb.get()
                    S.xcb = xcbt[:, :N]
                    cp(S.xcb, S.xc, r=[S.xck], w=[S.xcbk])

                def gates():
                    mm(gb_[0][:, :N], wblk[:, 0, j, :], S.xcb, True, True, r=["wblk", S.xcbk], w=[gk_[0]])
                    mm(gb_[1][:, :N], wblk[:, 1, j, :], S.xcb, True, True, r=["wblk", S.xcbk], w=[gk_[1]])

                def back():
                    trt, trk = ring.get()
                    tr = trt[:, :N]
                    act(tr, gb_[0][:, :N], AF.Tanh, bias=pv2(D_HBA + j), scale=0.5, r=[gk_[0], "PV2"], w=[trk])
                    tit, tik = ring.get()
                    ti = tit[:, :N]
                    act(ti, gb_[1][:, :N], AF.Tanh, bias=pv2(D_HBI + j), scale=0.5, r=[gk_[1], "PV2"], w=[tik])
                    at, ak = ring.get()
                    a = at[:, :N]
                    act(a, tr, AF.Exp, bias=pv2(D_CH + j), scale=pv2(D_CH + j), r=[trk, "PV2"], w=[ak])
                    act(tr, tr, AF.Exp, bias=pv2(D_C + j), scale=pv2(D_C + j), r=[trk, "PV2"], w=[trk])
                    act(tr, tr, AF.Ln, bias=ONE, scale=-1.0, r=[trk, "cst"], w=[trk])
                    act(tr, tr, AF.Exp, scale=0.5, r=[trk], w=[trk])
                    stt(ti, ti, 1.0, S.xc, ALU.add, ALU.mult, r=[tik, S.xck], w=[tik])
                    stt(ti, ti, 0.5, tr, ALU.mult, ALU.mult, r=[tik, trk], w=[tik])
                    if nseq == 1:
                        g.op(DVE, lambda e: e.tensor_tensor_scan(
                            hold4[:, jj, :N], a, ti, h0(j)[:, 0:1], ALU.mult, ALU.add),
                            reads=[ak, tik, h0k], writes=[("hold4", jj)])
                    else:
                        smt, smk = small.get()
                        tt(smt[:, 0:nseq].unsqueeze(2), V(a)[:, :, 0:1], h0(j).unsqueeze(2), ALU.mult,
                           r=[ak, h0k], w=[smk])
                        tt(V(ti)[:, :, 0:1], V(ti)[:, :, 0:1], smt[:, 0:nseq].unsqueeze(2), ALU.add,
                           r=[tik, smk], w=[tik])
                        g.op(DVE, lambda e: e.memset(V(a)[:, :, 0:1], 0.0), reads=[smk], writes=[ak])
                        g.op(DVE, lambda e: e.tensor_tensor_scan(hold4[:, jj, :N], a, ti, 0.0, ALU.mult, ALU.add),
                             reads=[ak, tik], writes=[("hold4", jj)])
                    cp(h0(j), V(hold4[:, jj, :N])[:, :, tps - 1], r=[("hold4", jj)], w=[h0k])

                return [(1, front), (2, gates), (3, back)]

            def st_ga(S):
                j = S.c
                jj = j % 4

                def post():
                    tgt, tgk = ring.get()
                    tg = tgt[:, :N]
                    act(tg, S.zp, AF.Tanh, scale=0.5, r=[S.zk], w=[tgk])
                    stt(tg, tg, 1.0, S.zp, ALU.add, ALU.mult, r=[tgk, S.zk], w=[tgk])
                    stt(yb[:, j, :N], tg, 0.5, hold4[:, jj, :N], ALU.mult, ALU.mult,
                        r=[tgk, ("hold4", jj)], w=[("y", j)])
                return [(4, post)]

            def st_gla(S):
                def post():
                    act(hold4[:, S.c, :N], S.zp, AF.Copy, r=[S.zk], w=[("hold4", S.c)])
                return [(1, post)]

            def st_glb(S):
                jj = S.c
                cb_ps = psB[1 + (jj % 2)]
                cbpk = ("psB", 1 + (jj % 2))

                def front():
                    tbt, tbk = ring.get()
                    tb = tbt[:, :N]
                    sigm(tb, S.zp, [S.zk], tbk)
                    get_, gk = glu_ext.get()
                    ge = get_[:, 0:nseq * (30 + tps)].rearrange("p (s t) -> p s t", t=30 + tps)
                    gbt, S.gbk = glu_bf.get()
                    S.geb = gbt[:, 0:nseq * (30 + tps)].rearrange("p (s t) -> p s t", t=30 + tps)
                    cp(ge[:, :, 0:30], hb(jj), r=[hbk], w=[gk])
                    tt(ge[:, :, 30:30 + tps], V(hold4[:, jj, :N]), V(tb), ALU.mult,
                       r=[tbk, ("hold4", jj)], w=[gk])
                    cp(hb(jj), ge[:, :, tps:tps + 30], r=[gk], w=[hbk])
                    cp(gbt[:, 0:nseq * (30 + tps)], get_[:, 0:nseq * (30 + tps)], r=[gk], w=[S.gbk])
                    S.dgt, S.dgk = dgw.get()
                    tt(S.dgt[:], ident[:].unsqueeze(1).to_broadcast([128, 31, 128]),
                       PV[:, l, O_CBW + jj * 31:O_CBW + jj * 31 + 31].unsqueeze(2).to_broadcast([128, 31, 128]),
                       ALU.mult, r=["ident", "PV"], w=[S.dgk])

                def conv():
                    for kk in range(31):
                        mm(V(cb_ps[:, :N]), S.dgt[:, kk, :], S.geb[:, :, kk:kk + tps], kk == 0, kk == 30,
                           r=[S.dgk, S.gbk], w=[cbpk])

                def post_conv():
                    cbk = ("cb4", jj)
                    act(cb4[:, jj, :N], cb_ps[:, :N], AF.Identity, bias=pvs(O_CBB + jj), r=[cbpk, "PV"], w=[cbk])
                    cbf, S.cbfk = ringb.get()
                    S.cbf = cbf
                    act(cbf[:, :N], cb4[:, jj, :N], AF.Copy, r=[cbk], w=[S.cbfk])
                    cbs, S.cbsk = ringb.get()
                    S.cbs = cbs
                    act(cbs[:, :N], cb4[:, jj, :N], AF.Square, r=[cbk], w=[S.cbsk])

                def stats():
                    mm(psB[0][:, :N], ones_b[:], S.cbf[:, :N], jj == 0, jj == 3, r=[S.cbfk, "ones"], w=[("psB", 0)])
                    mm(psB[3][:, :N], ones_b[:], S.cbs[:, :N], jj == 0, jj == 3, r=[S.cbsk, "ones"], w=[("psB", 3)])

                def fin():
                    m2t, m2k = ring.get()
                    m2 = m2t[:, :N]
                    act(m2, psB[0][:, :N], AF.Square, r=[("psB", 0)], w=[m2k])
                    act(meanB[:, :N], psB[0][:, :N], AF.Copy, r=[("psB", 0)], w=["meanB"])
                    tt(m2, psB[3][:, :N], m2, ALU.subtract, r=[("psB", 3), m2k], w=[m2k])
                    ts(m2, m2, 0.0, EPS, ALU.max, ALU.add, r=[m2k], w=[m2k])
                    S.m2, S.m2k = m2, m2k

                def fin_b():
                    rsqrt_act(rstdB[:, :N], S.m2, [S.m2k], "rstdB")

                st = [(1, front), (2, conv), (3, post_conv), (4, stats)]
                if jj == 3:
                    st += [(5, fin), (6, fin_b)]
                return st

            def st_gb(S):
                jj = S.c

                def pre():
                    S.tgbk = ("hold4", jj)
                    S.tgb = hold4[:, jj, :N]
                    sigm(S.tgb, S.zp, [S.zk], S.tgbk)
                    tt(S.tgb, S.tgb, S.zp, ALU.mult, r=[S.tgbk, S.zk], w=[S.tgbk])

                def post_a():
                    xct, S.xck = ring.get()
                    S.xc = xct[:, :N]
                    tt(S.xc, cb4[:, jj, :N], meanB[:, :N], ALU.subtract, r=[("cb4", jj), "meanB"], w=[S.xck])
                    tt(S.xc, S.xc, rstdB[:, :N], ALU.mult, r=[S.xck, "rstdB"], w=[S.xck])

                def post_b():
                    xc, xck = S.xc, S.xck
                    lnt, lnk = ring.get()
                    ln = lnt[:, :N]
                    act(ln, xc, AF.Identity, bias=pvs(O_LBB + jj), scale=pvs(O_LBG + jj), r=[xck, "PV"], w=[lnk])
                    sigm(xc, ln, [lnk], xck)
                    tt(ln, ln, xc, ALU.mult, r=[xck, lnk], w=[lnk])
                    tt(yb[:, 8 + jj, :N], ln, S.tgb, ALU.mult, r=[lnk, S.tgbk], w=[("y", 8 + jj)])
                return [(1, pre), (7, post_a), (8, post_b)]

            for g4 in range(4):
                zitems("xa", g4, st_xa)
                zitems("ga", g4, st_ga)
            for gx in range(2):
                zitems("gla", gx, st_gla)
            for gx in range(2):
                zitems("glb", gx, st_glb)
            for gx in range(2):
                zitems("gb", gx, st_gb)
            nq = N // 128

            def vz():
                for hv in range(2):
                    c0 = IN_OFF["v"] + hv * WG
                    wt, wk = load_w(wl[:, c0:c0 + WG], tag=(l, c0 // WG))
                    for q in range(nq):
                        for k in range(KC):
                            mm(psA[q][:, hv * WG:(hv + 1) * WG], xn[:, k, q * 128:(q + 1) * 128], wt[:, k, :],
                               k == 0, k == KC - 1, r=[wk, ("xn", k)], w=[("psA", q)])
                zcnt[0] = nq

            def vnorm(q):
                zp, zk = psA[q], ("psA", q)
                s6, s6k = small.get()
                g.op(DVE, lambda e: e.bn_stats(s6[:, 0:6], zp[:]), reads=[zk], writes=[s6k])
                mv, mvk = small.get()
                g.op(DVE, lambda e: e.bn_aggr(mv[:, 0:2], s6[:, 0:6]), reads=[s6k], writes=[mvk])
                rsqrt_act(mv[:, 2:3], mv[:, 1:2], [mvk], mvk, bias=EPSC)
                vh, vhk = ring.get()
                ts(vh[:], zp[:], mv[:, 0:1], mv[:, 2:3], ALU.subtract, ALU.mult, r=[zk, mvk], w=[vhk])
                tt(vh[:], vh[:], lncb[:, 0, :], ALU.mult, r=[vhk, "lncb"], w=[vhk])
                if cx.sample:
                    tt(vh[:], vh[:], lncb[:, 1, :], ALU.add, r=[vhk, "lncb"], w=[vhk])
                    dma(SP, gvs[l], vh[:], r=[vhk], key=("yout", vhk))
                    act(vn[:, q, :], vh[:], AF.Copy, r=[vhk], w=[("vn", q)])
                else:
                    tt(vn[:, q, :], vh[:], lncb[:, 1, :], ALU.add, r=[vhk, "lncb"], w=[("vn", q)])

            def st_u(S):
                def post():
                    act(hold4[:, S.c, :N], S.zp, AF.Copy, r=[S.zk], w=[("hold4", S.c)])
                return [(4, post)]

            def st_gc(S):
                jj = S.c
                mb = psB[1 + (jj % 2)]
                mbk = ("psB", 1 + (jj % 2))

                def pre():
                    S.tgck = ("cb4", jj)
                    S.tgc = cb4[:, jj, :N]
                    sigm(S.tgc, S.zp, [S.zk], S.tgck)
                    tt(S.tgc, S.tgc, S.zp, ALU.mult, r=[S.tgck, S.zk], w=[S.tgck])

                def mix():
                    for q in range(nq):
                        for h2 in range(2):
                            hd = 2 * jj + h2
                            wmix = BD[:, hd, :] if cx.sample else wsT[:, hd, :]
                            mm(mb[h2 * 64:(h2 + 1) * 64, q * 128:(q + 1) * 128],
                               vn[:, q, hd * 64:(hd + 1) * 64], wmix, True, True,
                               r=[("vn", q), "BD" if cx.sample else "wsT"], w=[mbk])

                def post_mix():
                    m1t, m1k = ring.get()
                    m1 = m1t[:, :N]
                    if cx.sample:
                        bias_bc = bsT[:, jj, 0:ST].unsqueeze(1).to_broadcast([128, NSQ, ST])
                        tt(V(m1), V(mb[:, :N]), bias_bc, ALU.add, r=[mbk, "bsT"], w=[m1k])
                    else:
                        tt(m1.rearrange("p (q t) -> p q t", t=128), mb[:, :N].rearrange("p (q t) -> p q t", t=128),
                           bsT[:, jj, :].unsqueeze(1).to_broadcast([128, nq, 128]), ALU.add,
                           r=[mbk, "bsT"], w=[m1k])
                    tt(m1, m1, hold4[:, jj, :N], ALU.mult, r=[m1k, ("hold4", jj)], w=[m1k])
                    tt(yb[:, 12 + jj, :N], m1, S.tgc, ALU.mult, r=[m1k, S.tgck], w=[("y", 12 + jj)])
                return [(1, pre), (2, mix), (3, post_mix)]

            pipe.add([(0, vz), (1, lambda: [vnorm(q) for q in range(nq)])])
            itemno[0] += 1
            for gx in range(2):
                zitems("u", gx, st_u)
            for gx in range(2):
                zitems("gc", gx, st_gc)
            pipe.run()

            for og in range(D // WG):
                box = {}
                for m4 in range(CPG):
                    S = Cx()
                    S.m = og * CPG + m4

                    def zfn(S=S, m4=m4, og=og, box=box):
                        if m4 == 0:
                            box["w"] = load_w(w_out[l][:, og * WG:(og + 1) * WG], tag=(l, 20 + og))
                        wt, wk = box["w"]
                        b = zcnt[0] % 4
                        zcnt[0] += 1
                        S.zp, S.zk = psA[b][:, :N], ("psA", b)
                        for k in range(KC):
                            mm(S.zp, wt[:, k, m4 * 128:(m4 + 1) * 128], yb[:, k, :N], k == 0, k == KC - 1,
                               r=[wk, ("y", k)], w=[S.zk])

                    def resid(S=S):
                        m = S.m
                        if cx.sample:
                            ot, otk = ring.get()
                            tt(V(ot[:, :N]), V(S.zp), bcs(32 + m), ALU.mult, r=[S.zk, "mod"], w=[otk])
                            tt(xT[:, m, :N], xT[:, m, :N], ot[:, :N], ALU.add, r=[otk, ("x", m)], w=[("x", m)])
                        else:
                            stt(xT[:, m, :N], S.zp, modT[:, l, 32 + m, 0:1], xT[:, m, :N], ALU.mult, ALU.add,
                                r=[S.zk, ("x", m), "mod"], w=[("x", m)])
                        sq, S.sk = ringb.get()
                        S.sq = sq
                        act(sq[:, :N], xT[:, m, :N], AF.Square, r=[("x", m)], w=[S.sk])

                    def ssq(S=S):
                        mm(psB[0][:, :N], ones_d[:], S.sq[:, :N], S.m == 0, S.m == KC - 1,
                           r=[S.sk, "ones"], w=[("psB", 0)])
                    pipe.add([(0, zfn), (1, resid), (2, ssq)])
            pipe.run()
            rms_fin(cx)
            if cx.sample:
                dma(SP, lcs[:, l], sst_a[:], r=["sst_a"], key="sst_a")
                dma(SP, lhs[:, l], sst_h[:], r=["sst_h"], key="sst_h")
                dma(SP, ccs[:, l], sst_c[:], r=["sst_c"], key="sst_c")

        def run_tile(cx, load_fn, store_fn):
            load_fn()
            for k in range(KC):
                rms_sq(cx, k, k == 0, k == KC - 1)
            rms_fin(cx)
            for l in range(L):
                run_layer(cx, l)
            stg = [xn[:].rearrange("p k t -> p (k t)").bitcast(F32), yb[:].rearrange("p k t -> p (k t)").bitcast(F32)]
            for k in range(KC):
                buf = stg[k // 8]
                c = k % 8
                t = buf[:, c * T:c * T + cx.N]
                key = "xn" if k < 8 else "y"
                stt(t, xT[:, k, :cx.N], FG[:, k:k + 1], rstd[:, :cx.N], ALU.mult, ALU.mult,
                    r=[("x", k), "rstd", "FG"], w=[(key, 2 * c), (key, 2 * c + 1)])
                if k % 4 == 3:
                    c0 = c - 3
                    src = buf[:, c0 * T:(c0 + 4) * T].rearrange("p (k t) -> p k t", t=T)[:, :, :cx.N]
                    store_fn(k // 4, src, [(key, 2 * c0 + i) for i in range(8)])

        QS = (SP, ACT)
        for p in range(NTILE):
            wmode[0] = "store" if (with_sample and p == NTILE - 1) else "normal"
            run_tile(PCX,
                     lambda p=p: [dma(QS[g4 % 2], xT[:, 4 * g4:4 * g4 + 4, :],
                                      xpT[g4 * 512:(g4 + 1) * 512, p * T:(p + 1) * T].rearrange("(k p) t -> p k t", p=128),
                                      w=[("x", 4 * g4 + i) for i in range(4)], key=("xin", g4)) for g4 in range(4)],
                     lambda g4, src, keys, p=p: dma(QS[g4 % 2],
                                                    ypT[g4 * 512:(g4 + 1) * 512, p * T:(p + 1) * T].rearrange(
                                                        "(k p) t -> p k t", p=128),
                                                    src, r=keys, key=("yst", g4)))
        if with_sample:
            dma(POOL, mask8[:], mask8_d, w=["mask8"], key="mask8")
            wmode[0] = "cached"
            run_tile(SCX,
                     lambda: [dma(QS[g4 % 2], xT[:, 4 * g4:4 * g4 + 4, :SCX.N], xsT[:, 4 * g4:4 * g4 + 4, :],
                                  w=[("x", 4 * g4 + i) for i in range(4)], key=("xin", g4)) for g4 in range(4)],
                     lambda g4, src, keys: dma(QS[g4 % 2], ysT[:, 4 * g4:4 * g4 + 4, :], src, r=keys, key=("yst", g4)))

        dma(SP, lcp, hist_a[:], r=["hist_a"], key="sout")
        dma(SP, lhp, hst[:], r=["hst"], key="sout")
        dma(SP, ccp, hist_b[:], r=["hist_b"], key="sout")

        emit_graph(nc, g, st)
        build_program.sbuf_left = nc.sbuf_bytes_remaining
    return nc


_NC_CACHE = {}


def _pack_pvec(norm_g, b_ada, conv_a_w, conv_a_b, lru_ba, lru_bi, lru_lam, conv_b_w, conv_b_b, ln_b_g, ln_b_b):
    pv = np.zeros((128, L, NP), np.float32)
    fm = lambda v, nch: v.reshape(nch, 128).T
    for l in range(L):
        pv[:, l, O_NG:O_NG + 16] = fm(norm_g[l], 16)
        pv[:, l, O_BADA:O_BADA + 48] = fm(b_ada[l], 48)
        pv[:, l, O_CAW:O_CAW + 32] = conv_a_w[l].reshape(4, 8, 128).transpose(2, 1, 0).reshape(128, 32)
        pv[:, l, O_CAB:O_CAB + 8] = fm(conv_a_b[l], 8)
        pv[:, l, O_BA:O_BA + 8] = fm(lru_ba[l], 8)
        pv[:, l, O_BI:O_BI + 8] = fm(lru_bi[l], 8)
        pv[:, l, O_LAM:O_LAM + 8] = fm(lru_lam[l], 8)
        pv[:, l, O_CBW:O_CBW + 124] = conv_b_w[l].reshape(31, 4, 128).transpose(2, 1, 0).reshape(128, 124)
        pv[:, l, O_CBB:O_CBB + 4] = fm(conv_b_b[l], 4)
        pv[:, l, O_LBG:O_LBG + 4] = fm(ln_b_g[l], 4)
        pv[:, l, O_LBB:O_LBB + 4] = fm(ln_b_b[l], 4)
    return pv


def _pack_wblk(wa, wi):
    out = np.zeros((L, 128, 2, 8, 128), np.float32)
    for which, w in enumerate((wa, wi)):
        for j in range(8):
            for half in range(2):
                out[:, half * 64:(half + 1) * 64, which, j, half * 64:(half + 1) * 64] = w[:, 2 * j + half]
    return out


def _pack_bsx(bs):
    out = np.zeros((L, 128, 4, 128), np.float32)
    for jj in range(4):
        for half in range(2):
            out[:, half * 64:(half + 1) * 64, jj, :] = bs[:, 2 * jj + half][:, None, :]
    return out


def _pack_ws8x(ws):
    out = np.zeros((L, 128, 8, 128), np.float32)
    blk = ws[:, :, :ST, :ST].transpose(0, 3, 1, 2)
    for sq in range(NSQ):
        out[:, sq * ST:(sq + 1) * ST, :, sq * ST:(sq + 1) * ST] = blk
    return out


def _mask8():
    m = np.zeros((128, 128), np.float32)
    for sq in range(NSQ):
        for tp in range(ST):
            m[sq * ST + tp, sq * ST + tp:(sq + 1) * ST] = 1.0
    return m


def kernel(x_prompt, x_sample, c_prompt, c_sample, state_lru_conv, state_lru_h, state_ccm_conv,
           norm_g, w_ada, b_ada, w_in, conv_a_w, conv_a_b, lru_wa, lru_ba, lru_wi, lru_bi, lru_lam,
           conv_b_w, conv_b_b, ln_b_g, ln_b_b, ln_c_g, ln_c_b, gmlp_ws, gmlp_bs, w_out, final_g):
    f = lambda a: np.ascontiguousarray(np.asarray(a, dtype=np.float32))
    x_prompt, x_sample, c_prompt, c_sample = f(x_prompt), f(x_sample), f(c_prompt), f(c_sample)
    state_lru_conv, state_lru_h, state_ccm_conv = f(state_lru_conv), f(state_lru_h), f(state_ccm_conv)
    with_sample = True
    if "nc" not in _NC_CACHE:
        _NC_CACHE["nc"] = build_program(with_sample)
    nc = _NC_CACHE["nc"]
    pv = _pack_pvec(*[np.asarray(a, np.float32) for a in
                      (norm_g, b_ada, conv_a_w, conv_a_b, lru_ba, lru_bi, lru_lam, conv_b_w, conv_b_b, ln_b_g, ln_b_b)])
    fg = f(np.asarray(final_g, np.float32).reshape(16, 128).T)
    shared = {
        "pvec": pv, "fgv": fg, "w_ada": f(w_ada), "w_in": f(w_in), "w_out": f(w_out),
        "wblk": _pack_wblk(np.asarray(lru_wa, np.float32), np.asarray(lru_wi, np.float32)),
        "wsT": f(np.asarray(gmlp_ws, np.float32).transpose(0, 1, 3, 2)),
        "bsx": _pack_bsx(np.asarray(gmlp_bs, np.float32)),
        "ws8x": _pack_ws8x(np.asarray(gmlp_ws, np.float32)), "mask8": _mask8(),
        "lnc": f(np.concatenate([np.asarray(ln_c_g), np.asarray(ln_c_b)], axis=1).reshape(L, 1, 1024)),
    }
    REAL = [0, 1, 4, 5]
    zx = np.zeros((D, SEQ), np.float32)
    in_maps = []
    for c in range(8):
        if c in REAL:
            b = REAL.index(c)
            cp_ = c_prompt[b:b + 1]
            xp_ = f(x_prompt[b].T)
        else:
            cp_ = np.zeros((1, D), np.float32)
            xp_ = zx
        cs = np.concatenate([cp_, c_sample[NSQ * c:NSQ * (c + 1)]], axis=0)
        m = dict(shared)
        m["xpT"] = xp_
        m["cT"] = f(cs.T.reshape(KC, 128, NS).transpose(1, 0, 2))
        sl = slice(NSQ * c, NSQ * (c + 1))
        m["xsT"] = f(x_sample[sl].reshape(NSQ * ST, KC, 128).transpose(2, 1, 0))
        m["slc"] = f(state_lru_conv[:, sl].reshape(L, NSQ, 3, 8, 128).transpose(4, 0, 3, 1, 2))
        m["slh"] = f(state_lru_h[:, sl].reshape(L, NSQ, 8, 128).transpose(3, 0, 2, 1))
        m["scc"] = f(state_ccm_conv[:, sl].reshape(L, NSQ, 30, 4, 128).transpose(4, 0, 3, 1, 2))
        in_maps.append(m)
    res = run_bass_kernel_spmd(nc, in_maps, core_ids=list(range(8)))
    R = res.results
    B = 4
    y_prompt = np.stack([R[REAL[b]]["ypT"].T for b in range(B)])
    lc_p = np.stack([R[REAL[b]]["lcp"].transpose(1, 3, 2, 0).reshape(L, 3, 1024) for b in range(B)], axis=1)
    lh_p = np.stack([R[REAL[b]]["lhp"].transpose(1, 2, 0).reshape(L, 1024) for b in range(B)], axis=1)
    cc_p = np.stack([R[REAL[b]]["ccp"].transpose(1, 3, 2, 0).reshape(L, 30, 512) for b in range(B)], axis=1)
    y_sample = np.concatenate([R[c]["ysT"].transpose(2, 1, 0).reshape(NSQ, ST, D) for c in range(8)], axis=0)
    lc_s = np.concatenate([R[c]["lcs"].transpose(1, 3, 4, 2, 0).reshape(L, NSQ, 3, 1024) for c in range(8)], axis=1)
    lh_s = np.concatenate([R[c]["lhs"].transpose(1, 3, 2, 0).reshape(L, NSQ, 1024) for c in range(8)], axis=1)
    cc_s = np.concatenate([R[c]["ccs"].transpose(1, 3, 4, 2, 0).reshape(L, NSQ, 30, 512) for c in range(8)], axis=1)
    gv_s = np.concatenate([R[c]["gvs"].reshape(L, NSQ, ST, 512) for c in range(8)], axis=1)
    y_sample, lc_s, lh_s, cc_s, gv_s = [np.ascontiguousarray(a) for a in (y_sample, lc_s, lh_s, cc_s, gv_s)]
    return (np.ascontiguousarray(y_prompt), y_sample, np.ascontiguousarray(lc_p), np.ascontiguousarray(lh_p),
            np.ascontiguousarray(cc_p), lc_s, lh_s, cc_s, gv_s)
```
